# Optimizing a Trainium2 kernel written in Bass

```python
import math
import jax, jax.numpy as jnp
from jax import lax
import numpy as np

D_MODEL = 2048
BATCH = 2
SEQ = 8192
DEPTH = 1
DEC_BATCH = 8
DEC_SEQ = 4096
PAST_LEN = 128

N_MEM = 256
MLA_HEADS = 12
Q_LORA = 512
KV_LORA = 512
QK_NOPE = 128
QK_ROPE = 64
V_HEAD = 128
ROPE_THETA = 10000.0
DIL_PAIRS = ((128, 1), (512, 4), (2048, 16))
DIL_GROUPS = 3
DIL_HEADS_PER_GROUP = 4
DIL_HEADS = DIL_GROUPS * DIL_HEADS_PER_GROUP
DIL_HEAD_DIM = 128
X_HEADS = 4
X_HEAD_DIM = 256
NUM_BUCKETS = 32
MAX_DISTANCE = 1024
D_FF = 4 * D_MODEL
N_BRANCH = 3
Q_BLOCK = 128
EPS = 1e-6
NEG_INF = -1e30

SPLIT_SIZES = (Q_LORA, KV_LORA, QK_ROPE, 3 * DIL_HEADS * DIL_HEAD_DIM, X_HEADS * X_HEAD_DIM, N_BRANCH * D_MODEL)
D_IN = Q_LORA + KV_LORA + QK_ROPE + 3 * DIL_HEADS * DIL_HEAD_DIM + X_HEADS * X_HEAD_DIM + N_BRANCH * D_MODEL

kernel_name = "hybrid_mla_dilated_memory_encoder"


def rms_norm(x, g):
    x32 = x.astype(jnp.float32)
    y = x32 * lax.rsqrt(jnp.mean(x32 * x32, axis=-1, keepdims=True) + EPS)
    return (y * g.astype(jnp.float32)).astype(x.dtype)


def apply_rope(x, pos):
    half = QK_ROPE // 2
    inv = 1.0 / (ROPE_THETA ** (jnp.arange(half, dtype=jnp.float32) / half))
    ang = pos.astype(jnp.float32)[:, None] * inv[None, :]
    cos = jnp.cos(ang)[:, None, :]
    sin = jnp.sin(ang)[:, None, :]
    x32 = x.astype(jnp.float32)
    x1, x2 = x32[..., :half], x32[..., half:]
    return jnp.concatenate([x1 * cos - x2 * sin, x2 * cos + x1 * sin], axis=-1).astype(x.dtype)


def t5_bucket(rel):
    nb = NUM_BUCKETS // 2
    ret = (rel > 0).astype(np.int32) * nb
    n = np.abs(rel)
    max_exact = nb // 2
    large = max_exact + (np.log(np.maximum(n, 1) / max_exact) / np.log(MAX_DISTANCE / max_exact)
                         * (nb - max_exact)).astype(np.int32)
    large = np.minimum(large, nb - 1)
    return (ret + np.where(n < max_exact, n, large)).astype(np.int32)


def mla_attention(c_q, c_kv, k_rope, pos, g_qn, w_uq, g_kvn, w_ukv):
    B, S, _ = c_q.shape
    q = (rms_norm(c_q, g_qn) @ w_uq).reshape(B, S, MLA_HEADS, QK_NOPE + QK_ROPE)
    q = jnp.concatenate([q[..., :QK_NOPE], apply_rope(q[..., QK_NOPE:], pos)], axis=-1)
    kv = (rms_norm(c_kv, g_kvn) @ w_ukv).reshape(B, S, MLA_HEADS, QK_NOPE + V_HEAD)
    k_nope, v = kv[..., :QK_NOPE], kv[..., QK_NOPE:]
    k_pe = jnp.broadcast_to(apply_rope(k_rope[:, :, None, :], pos), (B, S, MLA_HEADS, QK_ROPE))
    k = jnp.concatenate([k_nope, k_pe], axis=-1)
    scale = (QK_NOPE + QK_ROPE) ** -0.5

    def block(q0):
        qb = lax.dynamic_slice_in_dim(q, q0, Q_BLOCK, axis=1)
        logits = jnp.einsum('bqhd,bkhd->bhqk', qb, k).astype(jnp.float32) * scale
        p = jax.nn.softmax(logits, axis=-1)
        return jnp.einsum('bhqk,bkhd->bqhd', p.astype(v.dtype), v)

    out = lax.map(block, jnp.arange(S // Q_BLOCK) * Q_BLOCK)
    return out.transpose(1, 0, 2, 3, 4).reshape(B, S, MLA_HEADS * V_HEAD)


def dilated_attention(q, k, v, rel_bias):
    B, S = q.shape[0], q.shape[1]
    scale = DIL_HEAD_DIM ** -0.5
    groups = []
    for g, (w, r) in enumerate(DIL_PAIRS):
        n_side = (w // 2) // r
        offs = np.arange(-n_side, n_side + 1, dtype=np.int32) * r
        pad = n_side * r
        kp = jnp.pad(k[:, :, g], ((0, 0), (pad, pad), (0, 0), (0, 0)))
        vp = jnp.pad(v[:, :, g], ((0, 0), (pad, pad), (0, 0), (0, 0)))
        bias = rel_bias[t5_bucket(offs)][:, g * DIL_HEADS_PER_GROUP:(g + 1) * DIL_HEADS_PER_GROUP].T
        groups.append((offs, pad, q[:, :, g], kp, vp, bias.astype(jnp.float32)))

    def block(q0):
        i = q0 + jnp.arange(Q_BLOCK)
        outs, lses = [], []
        for offs, pad, qg, kp, vp, bias in groups:
            kpos = i[:, None] + offs[None, :]
            kb = jnp.take(kp, kpos + pad, axis=1)
            vb = jnp.take(vp, kpos + pad, axis=1)
            qb = lax.dynamic_slice_in_dim(qg, q0, Q_BLOCK, axis=1)
            logits = jnp.einsum('bqhd,bqkhd->bhqk', qb, kb).astype(jnp.float32) * scale
            logits = logits + bias[None, :, None, :]
            valid = (kpos >= 0) & (kpos < S)
            logits = jnp.where(valid[None, None], logits, NEG_INF)
            m = jnp.max(logits, axis=-1, keepdims=True)
            p = jnp.exp(logits - m)
            s = jnp.sum(p, axis=-1, keepdims=True)
            outs.append(jnp.einsum('bhqk,bqkhd->bqhd', (p / s).astype(vb.dtype), vb))
            lses.append((m + jnp.log(s))[..., 0])
        alpha = jax.nn.softmax(jnp.stack(lses, axis=-1), axis=-1).transpose(0, 2, 1, 3)
        out = sum(alpha[..., gi, None].astype(outs[gi].dtype) * outs[gi] for gi in range(DIL_GROUPS))
        return out

    out = lax.map(block, jnp.arange(S // Q_BLOCK) * Q_BLOCK)
    return out.transpose(1, 0, 2, 3, 4).reshape(B, S, DIL_HEADS_PER_GROUP * DIL_HEAD_DIM)


def memory_cross_attention(q, mem_n, w_mkv):
    B, S = q.shape[0], q.shape[1]
    kv = (mem_n @ w_mkv).reshape(B, mem_n.shape[1], 2, X_HEADS, X_HEAD_DIM)
    k, v = kv[:, :, 0], kv[:, :, 1]
    logits = jnp.einsum('bshd,bmhd->bhsm', q, k).astype(jnp.float32) * (X_HEAD_DIM ** -0.5)
    p = jax.nn.softmax(logits, axis=-1)
    out = jnp.einsum('bhsm,bmhd->bshd', p.astype(v.dtype), v)
    return out.reshape(B, S, X_HEADS * X_HEAD_DIM)


def encoder_layer(x, mem, pos, rel_bias, g_attn, w_in, g_qn, w_uq, g_kvn, w_ukv, g_mem, w_mkv,
                  w_b_mla, w_b_dil, w_b_mem, w_out, g_mlp, w_up, w_down):
    B, S, _ = x.shape
    h = rms_norm(x, g_attn)
    proj = h @ w_in
    cuts = np.cumsum(SPLIT_SIZES)[:-1].tolist()
    c_q, c_kv, k_rope, dil_qkv, x_q, gate_logits = jnp.split(proj, cuts, axis=-1)

    o_mla = mla_attention(c_q, c_kv, k_rope, pos, g_qn, w_uq, g_kvn, w_ukv)
    dil = dil_qkv.reshape(B, S, 3, DIL_GROUPS, DIL_HEADS_PER_GROUP, DIL_HEAD_DIM)
    o_dil = dilated_attention(dil[:, :, 0], dil[:, :, 1], dil[:, :, 2], rel_bias)
    o_mem = memory_cross_attention(x_q.reshape(B, S, X_HEADS, X_HEAD_DIM), rms_norm(mem, g_mem), w_mkv)

    gates = jax.nn.sigmoid(gate_logits.astype(jnp.float32)).astype(x.dtype).reshape(B, S, N_BRANCH, D_MODEL)
    merged = (gates[:, :, 0] * (o_mla @ w_b_mla)
              + gates[:, :, 1] * (o_dil @ w_b_dil)
              + gates[:, :, 2] * (o_mem @ w_b_mem))
    x = x + merged @ w_out
    h = rms_norm(x, g_mlp)
    x = x + jnp.square(jax.nn.relu(h @ w_up)) @ w_down
    return x


def encoder_trunk(x, mem, rel_bias, g_attn, w_in, g_qn, w_uq, g_kvn, w_ukv, g_mem, w_mkv,
                  w_b_mla, w_b_dil, w_b_mem, w_out, g_mlp, w_up, w_down, g_final):
    pos = jnp.arange(x.shape[1])
    for l in range(DEPTH):
        x = encoder_layer(x, mem, pos, rel_bias, g_attn[l], w_in[l], g_qn[l], w_uq[l], g_kvn[l], w_ukv[l],
                          g_mem[l], w_mkv[l], w_b_mla[l], w_b_dil[l], w_b_mem[l], w_out[l],
                          g_mlp[l], w_up[l], w_down[l])
    return rms_norm(x, g_final)


def setup_inputs(seed: int = 0) -> dict:
    key = jax.random.key(seed)
    ks = jax.random.split(key, 32)
    f32 = jnp.float32

    def w(k, fan_in, shape):
        return jax.random.normal(k, shape, f32) * (fan_in ** -0.5)

    def gain(k, shape):
        return 1.0 + 0.02 * jax.random.normal(k, shape, f32)

    L = DEPTH
    return {
        "x_prompt": jax.random.normal(ks[0], (BATCH, SEQ, D_MODEL), f32),
        "x_sample": jax.random.normal(ks[1], (DEC_BATCH, DEC_SEQ, D_MODEL), f32),
        "mem_prompt": jax.random.normal(ks[2], (BATCH, N_MEM, D_MODEL), f32),
        "mem_sample": jax.random.normal(ks[3], (DEC_BATCH, N_MEM, D_MODEL), f32),
        "rel_bias": 0.5 * jax.random.normal(ks[4], (NUM_BUCKETS, DIL_HEADS), f32),
        "g_attn": gain(ks[5], (L, D_MODEL)),
        "w_in": w(ks[6], D_MODEL, (L, D_MODEL, D_IN)),
        "g_q_norm": gain(ks[7], (L, Q_LORA)),
        "w_uq": w(ks[8], Q_LORA, (L, Q_LORA, MLA_HEADS * (QK_NOPE + QK_ROPE))),
        "g_kv_norm": gain(ks[9], (L, KV_LORA)),
        "w_ukv": w(ks[10], KV_LORA, (L, KV_LORA, MLA_HEADS * (QK_NOPE + V_HEAD))),
        "g_mem": gain(ks[11], (L, D_MODEL)),
        "w_mem_kv": w(ks[12], D_MODEL, (L, D_MODEL, 2 * X_HEADS * X_HEAD_DIM)),
        "w_b_mla": w(ks[13], MLA_HEADS * V_HEAD, (L, MLA_HEADS * V_HEAD, D_MODEL)),
        "w_b_dil": w(ks[14], DIL_HEADS_PER_GROUP * DIL_HEAD_DIM, (L, DIL_HEADS_PER_GROUP * DIL_HEAD_DIM, D_MODEL)),
        "w_b_mem": w(ks[15], X_HEADS * X_HEAD_DIM, (L, X_HEADS * X_HEAD_DIM, D_MODEL)),
        "w_out": w(ks[16], D_MODEL, (L, D_MODEL, D_MODEL)),
        "g_mlp": gain(ks[17], (L, D_MODEL)),
        "w_up": w(ks[18], D_MODEL, (L, D_MODEL, D_FF)),
        "w_down": w(ks[19], D_FF, (L, D_FF, D_MODEL)),
        "g_final": gain(ks[20], (D_MODEL,)),
    }


def reference(x_prompt, x_sample, mem_prompt, mem_sample, rel_bias, g_attn, w_in, g_q_norm, w_uq,
              g_kv_norm, w_ukv, g_mem, w_mem_kv, w_b_mla, w_b_dil, w_b_mem, w_out, g_mlp, w_up, w_down,
              g_final):
    y_prompt = encoder_trunk(x_prompt, mem_prompt, rel_bias, g_attn, w_in, g_q_norm, w_uq, g_kv_norm, w_ukv,
                             g_mem, w_mem_kv, w_b_mla, w_b_dil, w_b_mem, w_out, g_mlp, w_up, w_down, g_final)
    y_sample = encoder_trunk(x_sample, mem_sample, rel_bias, g_attn, w_in, g_q_norm, w_uq, g_kv_norm, w_ukv,
                             g_mem, w_mem_kv, w_b_mla, w_b_dil, w_b_mem, w_out, g_mlp, w_up, w_down, g_final)
    return (y_prompt, y_sample)
```

```python
import contextlib
import math
import numpy as np
import ml_dtypes
import concourse.bass as bass
import concourse.mybir as mybir
from concourse.bass_utils import run_bass_kernel_spmd

F32 = mybir.dt.float32
BF16 = mybir.dt.bfloat16
AF = mybir.ActivationFunctionType
ALU = mybir.AluOpType

D = 2048
H = 12
NBUCK = 32
EPS = 1e-6
DIL_R = (1, 4, 16)
PAD = 1024
MASKV = -30000.0
SC_MLA = float(192 ** -0.5)
SC_DIL = float(128 ** -0.5)
SC_MEM = float(256 ** -0.5)
ARENA_ELEMS = 104400


class Res:
    __slots__ = ("name", "last_w", "readers", "multi", "writers")

    def __init__(self, name, multi=False):
        self.name = name
        self.last_w = None
        self.readers = []
        self.multi = multi
        self.writers = []


class Op:
    __slots__ = ("eng", "fn", "reads", "writes", "dma_key", "group", "deps", "signal", "sem", "ticket")

    def __init__(self, eng, fn, reads, writes, dma_key, group):
        self.eng = eng
        self.fn = fn
        self.reads = reads
        self.writes = writes
        self.dma_key = dma_key
        self.group = group
        self.deps = []
        self.signal = dma_key is not None
        self.sem = None
        self.ticket = 0


class Sched:
    ENGS = ("pe", "act", "dve", "pool", "sp")

    def __init__(self, nc):
        self.nc = nc
        self.ops = []
        self.last_eng = {e: None for e in self.ENGS}
        self.last_dma = {}

    def op(self, eng, fn, reads=(), writes=(), dma_key=None, group=False):
        o = Op(eng, fn, tuple(reads), tuple(writes), dma_key, group)
        deps = {}
        raw = set()
        for r in o.reads:
            if r.multi:
                for p in r.writers:
                    deps[id(p)] = p
                    raw.add(id(p))
            elif r.last_w is not None:
                deps[id(r.last_w)] = r.last_w
                raw.add(id(r.last_w))
        for w in o.writes:
            if w.multi:
                w.writers.append(o)
                continue
            if w.last_w is not None:
                deps[id(w.last_w)] = w.last_w
            for p in w.readers:
                deps[id(p)] = p
            w.last_w = o
            w.readers = []
        for r in o.reads:
            if not r.multi:
                if o.dma_key is None:
                    r.readers = [p for p in r.readers if p.dma_key is not None or p.eng != o.eng]
                r.readers.append(o)
        for pid, p in deps.items():
            if p is o:
                continue
            if p.dma_key is None and o.dma_key is None and p.eng == o.eng:
                if o.eng == "pe":
                    continue
                if pid not in raw:
                    continue
            p.signal = True
            o.deps.append(p)
        self.ops.append(o)
        if dma_key is None:
            self.last_eng[eng] = o
        else:
            if not group:
                self.last_dma[dma_key] = o
        return o

    def barrier(self):
        targets = [p for p in self.last_eng.values() if p is not None] + list(self.last_dma.values())
        for e in self.ENGS:
            o = Op(e, None, (), (), None, False)
            o.signal = False
            for p in targets:
                if p.dma_key is None and p.eng == e:
                    continue
                p.signal = True
                o.deps.append(p)
            self.ops.append(o)
        self.last_dma = {}

    def emit(self):
        nc = self.nc
        eng_cnt = {e: 0 for e in self.ENGS}
        dma_cnt = {}
        for o in self.ops:
            if o.dma_key is not None:
                dma_cnt[o.dma_key] = dma_cnt.get(o.dma_key, 0) + 16
                o.sem = ("dma", o.dma_key)
                o.ticket = dma_cnt[o.dma_key]
            elif o.signal:
                eng_cnt[o.eng] += 1
                o.sem = ("eng", o.eng)
                o.ticket = eng_cnt[o.eng]
        for o in self.ops:
            if o.dma_key is not None and o.group:
                o.ticket = dma_cnt[o.dma_key]
        sem_names = [("eng", e) for e in self.ENGS] + [("dma", k) for k in dma_cnt]
        self.n_sems = len(sem_names)
        with contextlib.ExitStack() as es:
            sems = {}
            for i, k in enumerate(sem_names):
                sems[k] = es.enter_context(nc.semaphore("s%d" % i))
            block = es.enter_context(nc.Block())
            per_eng = {e: [o for o in self.ops if o.eng == e] for e in self.ENGS}

            def run(e, eng_obj):
                waited = {}
                for o in per_eng[e]:
                    for p in o.deps:
                        if waited.get(p.sem, 0) < p.ticket:
                            eng_obj.wait_ge(sems[p.sem], p.ticket)
                            waited[p.sem] = p.ticket
                    if o.fn is None:
                        continue
                    ins = o.fn(eng_obj)
                    if o.signal:
                        ins.then_inc(sems[o.sem], 16 if o.dma_key is not None else 1)
                if e == "sp":
                    for k, v in dma_cnt.items():
                        if waited.get(("dma", k), 0) < v:
                            eng_obj.wait_ge(sems[("dma", k)], v)

            @block.tensor
            def _(t):
                run("pe", t)

            @block.scalar
            def _(a):
                run("act", a)

            @block.vector
            def _(v):
                run("dve", v)

            @block.gpsimd
            def _(g):
                run("pool", g)

            @block.sync
            def _(s):
                run("sp", s)
        return eng_cnt, dma_cnt


class Tile:
    __slots__ = ("ap", "res")

    def __init__(self, ap, res):
        self.ap = ap
        self.res = res


class Arena:
    def __init__(self, arena_ap, nelems):
        self.a = arena_ap
        self.n = nelems
        self.off = 0
        self.mark = 0
        self.cnt = 0

    def alloc(self, shape, dt, name=None):
        nel = int(np.prod(shape[1:]))
        nb = nel * (4 if dt == F32 else 2)
        nb = (nb + 63) // 64 * 64
        o = self.off
        self.off += nb // 2
        assert self.off <= self.n, "arena overflow %d > %d" % (self.off, self.n)
        ap = self.a[0:shape[0], o:o + (nel * 2 if dt == F32 else nel)]
        if dt == F32:
            ap = ap.bitcast(F32)
        if len(shape) == 3:
            ap = ap.rearrange("p (a b) -> p a b", a=shape[1])
        elif len(shape) == 4:
            ap = ap.rearrange("p (a b c) -> p a b c", a=shape[1], b=shape[2])
        self.cnt += 1
        return Tile(ap, Res(name or ("t%d" % self.cnt)))

    def set_mark(self):
        self.mark = self.off

    def reset(self):
        self.off = self.mark


def t5_bucket(rel):
    nb = NBUCK // 2
    ret = (rel > 0).astype(np.int32) * nb
    n = np.abs(rel)
    max_exact = nb // 2
    large = max_exact + (np.log(np.maximum(n, 1) / max_exact) / np.log(1024 / max_exact)
                         * (nb - max_exact)).astype(np.int32)
    large = np.minimum(large, nb - 1)
    return (ret + np.where(n < max_exact, n, large)).astype(np.int32)


def onehot_tables():
    oh = np.zeros((3, 33, 384), np.float32)
    for g, r in enumerate(DIL_R):
        for i in range(384):
            jj = i - 191
            if -64 <= jj <= 64:
                b = int(t5_bucket(np.array([jj * r], dtype=np.int32))[0])
                oh[g, b, i] = 1.0
            else:
                oh[g, 32, i] = 1.0
    return oh


def rope_tables(pos):
    half = 32
    inv = (1.0 / (10000.0 ** (np.arange(half, dtype=np.float32) / half))).astype(np.float32)
    ang = pos.astype(np.float32)[None, :] * inv[:, None]
    cos = np.cos(ang).astype(np.float32)
    sin = np.sin(ang).astype(np.float32)
    cs = np.concatenate([cos, cos, cos, cos], axis=0)
    ss = np.concatenate([-sin, sin, -sin, sin], axis=0)
    return np.ascontiguousarray(cs), np.ascontiguousarray(ss)


def vb_cols(nq):
    return [nq // 128 + r for r in DIL_R]


def vbias_table(nq, q0, s_total):
    cols = []
    for g, r in enumerate(DIL_R):
        nb = nq // (128 * r)
        for c in range(r):
            for m in range(nb + 1):
                kk = np.arange(128)
                pidx = PAD + c - 64 * r + 128 * r * m + r * kk
                pos = q0 + pidx - PAD
                ok = (pos >= 0) & (pos < s_total)
                cols.append(np.where(ok, 0.0, MASKV).astype(np.float32))
    return np.ascontiguousarray(np.stack(cols, axis=1))


class JobSpec:
    def __init__(self, name, S, nq, nh):
        self.name = name
        self.S = S
        self.nq = nq
        self.nh = nh


def prep_job_inputs(spec, x_seq, mem, q0):
    s_total = x_seq.shape[0]
    nq = spec.nq
    assert spec.S == s_total
    order = np.concatenate([np.arange(q0, q0 + nq), np.arange(0, q0), np.arange(q0 + nq, s_total)])
    d = {}
    n = spec.name
    d["x_" + n] = np.ascontiguousarray(x_seq[order])
    if spec.nh:
        xh = np.zeros((2 * PAD, D), np.float32)
        lo = q0 - PAD
        for dst0, src0 in ((0, lo), (PAD, q0 + nq)):
            a = max(src0, 0)
            b = min(src0 + PAD, s_total)
            if b > a:
                xh[dst0 + (a - src0):dst0 + (b - src0)] = x_seq[a:b]
        d["xh_" + n] = xh
    d["mem_" + n] = np.ascontiguousarray(mem)
    cs, ss = rope_tables(order)
    d["cs_" + n] = cs
    d["ss_" + n] = ss
    d["vb_" + n] = vbias_table(nq, q0, s_total)
    return d


class Builder:
    def __init__(self, specs, dbg=False):
        self.specs = specs
        self.dbg = dbg
        nc = bass.Bass("TRN2", target_bir_lowering=False)
        self.nc = nc
        self.S = Sched(nc)
        self.ev = 0

    def dram_in(self, name, shape, dt=F32):
        return self.nc.dram_tensor(name, list(shape), dt, kind="ExternalInput").ap()

    def dram_out(self, name, shape, dt=F32):
        return self.nc.dram_tensor(name, list(shape), dt, kind="ExternalOutput").ap()

    def dram_scr(self, name, shape, dt=BF16):
        kind = "ExternalOutput" if (self.dbg and name in self.dbg) else "Internal"
        return self.nc.dram_tensor(name, list(shape), dt, kind=kind).ap()

    def declare(self):
        W = {}
        W["w_in"] = self.dram_in("w_in", [D, 12864])
        W["w_uq"] = self.dram_in("w_uq", [512, 2304])
        W["w_ukv"] = self.dram_in("w_ukv", [512, 3072])
        W["w_mem_kv"] = self.dram_in("w_mem_kv", [D, 2048])
        W["w_b_mla"] = self.dram_in("w_b_mla", [1536, D])
        W["w_b_dil"] = self.dram_in("w_b_dil", [512, D])
        W["w_b_mem"] = self.dram_in("w_b_mem", [1024, D])
        W["w_out"] = self.dram_in("w_out", [D, D])
        W["w_up"] = self.dram_in("w_up", [D, 8192])
        W["w_down"] = self.dram_in("w_down", [8192, D])
        for g in ("g_attn", "g_mem", "g_mlp", "g_final"):
            W[g] = self.dram_in(g, [1, D])
        W["g_q_norm"] = self.dram_in("g_q_norm", [1, 512])
        W["g_kv_norm"] = self.dram_in("g_kv_norm", [1, 512])
        W["rel_bias"] = self.dram_in("rel_bias", [NBUCK, H])
        W["oh"] = self.dram_in("oh", [3, 33, 384])
        W["antiid"] = self.dram_in("antiid", [128, 128])
        W["ident"] = self.dram_in("ident", [128, 128], BF16)
        self.W = W
        Z = {}
        Z["q"] = self.dram_scr("z_q", [D, 512])
        Z["kv"] = self.dram_scr("z_kv", [D, 512])
        Z["kr"] = self.dram_scr("z_kr", [D, 256])
        Z["dil"] = self.dram_scr("z_dil", [D, 4608])
        Z["xq"] = self.dram_scr("z_xq", [D, 1024])
        Z["g"] = self.dram_scr("z_g", [D, 16, 3, 128])
        Z["uqn"] = self.dram_scr("z_uqn", [512, 12, 128])
        Z["uqr"] = self.dram_scr("z_uqr", [512, 12, 64])
        Z["uqrs"] = self.dram_scr("z_uqrs", [512, 12, 64])
        Z["ukvk"] = self.dram_scr("z_ukvk", [512, 12, 128])
        Z["ukvv"] = self.dram_scr("z_ukvv", [512, 12, 128])
        Z["mkv"] = self.dram_scr("z_mkv", [D, 2048])
        Z["b"] = self.dram_scr("z_b", [3072, D])
        Z["out"] = self.dram_scr("z_out", [D, D])
        Z["up"] = self.dram_scr("z_up", [D, 8192])
        Z["down"] = self.dram_scr("z_down", [8192, D])
        self.Z = Z
        self.vp = self.dram_scr("vp", [12, 384], F32)
        self.Tsc = self.dram_scr("Tsc", [128, 2, 24, 128])
        self.jobs = []
        for sp in self.specs:
            n = sp.name
            J = {"spec": sp}
            J["x"] = self.dram_in("x_" + n, [sp.S, D])
            if sp.nh:
                J["xh"] = self.dram_in("xh_" + n, [sp.nh, D])
            J["mem"] = self.dram_in("mem_" + n, [256, D])
            J["cs"] = self.dram_in("cs_" + n, [128, sp.S])
            J["ss"] = self.dram_in("ss_" + n, [128, sp.S])
            J["vb"] = self.dram_in("vb_" + n, [128, sum(vb_cols(sp.nq))])
            J["y"] = self.dram_out("y_" + n, [sp.nq, D])
            J["QnT"] = self.dram_scr("QnT_" + n, [12, 128, sp.nq])
            J["QrT"] = self.dram_scr("QrT_" + n, [6, 128, sp.nq])
            J["KnT"] = self.dram_scr("KnT_" + n, [12, 128, sp.S])
            J["KrT"] = self.dram_scr("KrT_" + n, [128, sp.S])
            J["V"] = self.dram_scr("V_" + n, [12, 128, sp.S // 128, 128])
            J["dQT"] = self.dram_scr("dQT_" + n, [12, 128, sp.nq])
            J["dKT"] = self.dram_scr("dKT_" + n, [12, 128, sp.nq + 2 * PAD])
            J["dV"] = self.dram_scr("dV_" + n, [sp.nq + 2 * PAD, 1536])
            J["OmT"] = self.dram_scr("OmT_" + n, [12, 128, sp.nq])
            J["OdT"] = self.dram_scr("OdT_" + n, [4, 128, sp.nq])
            self.jobs.append(J)

    def load(self, dst, src_ap, key, reads=(), eng="sp", dst_ap=None, slow=False):
        d = dst.ap if dst_ap is None else dst_ap
        if slow:
            self.S.op(eng, lambda e: e.dma_start(out=d, in_=src_ap, allow_slow_non_contiguous=True), reads=reads,
                      writes=[dst.res], dma_key=key)
        else:
            self.S.op(eng, lambda e: e.dma_start(out=d, in_=src_ap), reads=reads, writes=[dst.res], dma_key=key)

    def store(self, dst_ap, src, key, writes=(), eng="pool", src_ap=None):
        s = src.ap if src_ap is None else src_ap
        self.S.op(eng, lambda e: e.dma_start(out=dst_ap, in_=s), reads=[src.res], writes=writes, dma_key=key)

    def mm(self, out_ap, out_res, lhsT_ap, rhs_ap, reads, start, stop):
        self.S.op("pe", lambda e: e.matmul(out_ap, lhsT=lhsT_ap, rhs=rhs_ap, start=start, stop=stop),
                  reads=reads, writes=[out_res])

    def evac(self, out_ap, out_res, in_ap, in_res, extra_reads=()):
        self.ev += 1
        if self.ev % 2 == 0:
            self.S.op("act", lambda e: e.copy(out=out_ap, in_=in_ap), reads=[in_res, *extra_reads], writes=[out_res])
        else:
            self.S.op("dve", lambda e: e.tensor_copy(out=out_ap, in_=in_ap), reads=[in_res, *extra_reads],
                      writes=[out_res])

    def bank(self):
        b = self.pring[self.pi % len(self.pring)]
        self.pi += 1
        return b

    def ring(self, name):
        lst = self.rings[name]
        i = self.ring_i.get(name, 0)
        self.ring_i[name] = i + 1
        return lst[i % len(lst)], "%s%d" % (name, i % len(lst))

    def mkring(self, name, n, shape, dt):
        self.rings[name] = [self.A.alloc(shape, dt, "%s%d" % (name, i)) for i in range(n)]
        self.ring_i[name] = 0

    def wblock(self, src_ap, kc, ncols, reads):
        t, key = self.ring("wr")
        v = t.ap[:, 0:kc * ncols].rearrange("p (c n) -> p c n", c=kc)
        self.S.op("sp", lambda e: e.dma_start(out=v, in_=src_ap), reads=reads, writes=[t.res], dma_key=key)
        return v, t.res

    def conv(self, dst, src, group):
        self.S.op("pool", lambda e: e.dma_start(out=dst, in_=src), writes=[self.r_w[group]],
                  dma_key="cv" + group, group=True)

    def conv_rows(self, dst, src, group, rows, step):
        for r0 in range(0, rows, step):
            self.conv(dst[r0:r0 + step], src[r0:r0 + step], group)

    def phase_W_early(self):
        W, Z = self.W, self.Z
        self.r_w = {"A1": Res("wA1", multi=True), "A2": Res("wA2", multi=True), "B": Res("wB", multi=True)}
        win = W["w_in"]
        self.conv(Z["q"], win[:, 0:512], "A1")
        self.conv(Z["kv"], win[:, 512:1024], "A1")
        self.conv(Z["kr"][:, 0:64], win[:, 1024:1088], "A1")
        self.conv(Z["kr"][:, 64:128], win[:, 1024:1088], "A1")
        for o in (128, 192):
            self.conv(Z["kr"][:, o:o + 32], win[:, 1056:1088], "A1")
            self.conv(Z["kr"][:, o + 32:o + 64], win[:, 1024:1056], "A1")
        uq = W["w_uq"].rearrange("k (h d) -> k h d", d=192)
        self.conv(Z["uqn"], uq[:, :, 0:128], "A1")
        self.conv(Z["uqr"], uq[:, :, 128:192], "A1")
        self.conv(Z["uqrs"][:, :, 0:32], uq[:, :, 160:192], "A1")
        self.conv(Z["uqrs"][:, :, 32:64], uq[:, :, 128:160], "A1")
        ukv = W["w_ukv"].rearrange("k (h d) -> k h d", d=256)
        self.conv(Z["ukvk"], ukv[:, :, 0:128], "A1")
        self.conv(Z["ukvv"], ukv[:, :, 128:256], "A1")
        self.conv_rows(Z["dil"], win[:, 1088:5696], "A2", D, 512)
        pend = []

        def add(dst, src, rows, step):
            for r0 in range(0, rows, step):
                pend.append((dst[r0:r0 + step], src[r0:r0 + step]))
        add(Z["xq"], win[:, 5696:6720], D, 512)
        for r0 in range(0, D, 256):
            for br in range(3):
                gsrc = win[r0:r0 + 256, 6720 + br * 2048:6720 + (br + 1) * 2048].rearrange("k (fc d) -> k fc d", fc=16)
                pend.append((Z["g"][r0:r0 + 256, :, br, :], gsrc))
        add(Z["mkv"], W["w_mem_kv"], D, 256)
        add(Z["b"][0:1536], W["w_b_mla"], 1536, 256)
        add(Z["b"][1536:2048], W["w_b_dil"], 512, 256)
        add(Z["b"][2048:3072], W["w_b_mem"], 1024, 256)
        add(Z["out"], W["w_out"], D, 256)
        add(Z["up"], W["w_up"], D, 64)
        add(Z["down"], W["w_down"], 8192, 256)
        self.pend = pend

    def drip(self, n):
        for _ in range(n):
            if self.pend:
                d, s = self.pend.pop(0)
                self.conv(d, s, "B")

    def norm_T(self, src_rows, gb, hT, xtiles=None):
        S = self.S
        for j in range(4):
            if xtiles is None:
                xt, key = self.ring("xr")
            else:
                xt, key = xtiles[j], "xres%d" % j
            self.load(xt, src_rows[j * 128:(j + 1) * 128, :], key)
            self.norm_T_sub(xt, gb, hT, j)

    def norm_T_sub(self, xt, gb, hT, j, ncols=512):
        S = self.S
        st, _ = self.ring("stat")
        hb, _ = self.ring("hb")
        junk = self.junk
        S.op("act", lambda e: e.activation(out=junk.ap, in_=xt.ap, func=AF.Square, scale=float(D ** -0.5),
                                           accum_out=st.ap[:, 0:1]),
             reads=[xt.res], writes=[junk.res, st.res])
        S.op("act", lambda e: e.activation(out=st.ap[:, 1:2], in_=st.ap[:, 0:1], func=AF.Sqrt, bias=EPS),
             reads=[st.res], writes=[st.res])
        S.op("dve", lambda e: e.reciprocal(out=st.ap[:, 2:3], in_=st.ap[:, 1:2]), reads=[st.res], writes=[st.res])
        S.op("dve", lambda e: e.scalar_tensor_tensor(out=hb.ap, in0=xt.ap, scalar=st.ap[:, 2:3], in1=gb.ap,
                                                     op0=ALU.mult, op1=ALU.mult),
             reads=[xt.res, st.res, gb.res], writes=[hb.res])
        for half in range(2):
            tb = self.tbanks[self.ti % 2]
            self.ti += 1
            for c in range(8):
                cc = half * 8 + c
                S.op("pe", lambda e, tb=tb, c=c, cc=cc: e.transpose(out=tb.ap[:, c, :],
                                                                   in_=hb.ap[:, cc * 128:(cc + 1) * 128],
                                                                   identity=self.ident.ap),
                     reads=[hb.res, self.ident.res], writes=[tb.res])
            self.evac(hT.ap[:, half * 8:(half + 1) * 8, j * 128:(j + 1) * 128], hT.res, tb.ap, tb.res)

    def norm_pre(self, src_rows, gb):
        S = self.S
        hbs = []
        for j in range(4):
            xt, key = self.ring("xr")
            self.load(xt, src_rows[j * 128:(j + 1) * 128, :], key)
            st, _ = self.ring("stat")
            hb, _ = self.ring("hb")
            S.op("act", lambda e, hb=hb, xt=xt, st=st: e.activation(out=hb.ap, in_=xt.ap, func=AF.Square,
                                                                    scale=float(D ** -0.5), accum_out=st.ap[:, 0:1]),
                 reads=[xt.res], writes=[hb.res, st.res])
            S.op("act", lambda e, st=st: e.activation(out=st.ap[:, 1:2], in_=st.ap[:, 0:1], func=AF.Sqrt, bias=EPS),
                 reads=[st.res], writes=[st.res])
            S.op("dve", lambda e, st=st: e.reciprocal(out=st.ap[:, 2:3], in_=st.ap[:, 1:2]), reads=[st.res],
                 writes=[st.res])
            S.op("dve", lambda e, hb=hb, xt=xt, st=st: e.scalar_tensor_tensor(out=hb.ap, in0=xt.ap, scalar=st.ap[:, 2:3],
                                                                            in1=gb.ap, op0=ALU.mult, op1=ALU.mult),
                 reads=[xt.res, st.res, gb.res], writes=[hb.res])
            hbs.append(hb)
        return hbs

    def norm_post(self, hbs, hT):
        S = self.S
        for j, hb in enumerate(hbs):
            for half in range(2):
                tb = self.tbanks[self.ti % 2]
                self.ti += 1
                for c in range(8):
                    cc = half * 8 + c
                    S.op("pe", lambda e, tb=tb, c=c, cc=cc, hb=hb: e.transpose(out=tb.ap[:, c, :],
                                                                              in_=hb.ap[:, cc * 128:(cc + 1) * 128],
                                                                              identity=self.ident.ap),
                         reads=[hb.res, self.ident.res], writes=[tb.res])
                self.evac(hT.ap[:, half * 8:(half + 1) * 8, j * 128:(j + 1) * 128], hT.res, tb.ap, tb.res)

    def setup_norm2(self):
        self.mkring("xr", 2, [128, D], F32)
        self.mkring("stat", 4, [128, 4], F32)
        self.mkring("hb", 4, [128, D], BF16)

    def latent_proj(self, hT, wv, wres, c32, csq):
        S = self.S
        for oc in range(4):
            b = self.bank()
            for k in range(16):
                self.mm(b.ap, b.res, wv[:, k, oc * 128:(oc + 1) * 128], hT.ap[:, k, :], [wres, hT.res], k == 0, k == 15)
            S.op("dve", lambda e, b=b, oc=oc: e.tensor_copy(out=c32.ap[:, oc, :], in_=b.ap), reads=[b.res],
                 writes=[c32.res])
            S.op("act", lambda e, oc=oc: e.activation(out=csq.ap[:, oc, :], in_=c32.ap[:, oc, :], func=AF.Square),
                 reads=[c32.res], writes=[csq.res])

    def latent_norm(self, gvec, c32, csq, rsb, cn):
        S = self.S
        b = self.bank()
        for oc in range(4):
            self.mm(b.ap, b.res, self.ones.ap, csq.ap[:, oc, :], [self.ones.res, csq.res], oc == 0, oc == 3)
        S.op("act", lambda e: e.activation(out=rsb.ap, in_=b.ap, func=AF.Sqrt, bias=EPS, scale=1.0 / 512),
             reads=[b.res], writes=[rsb.res])
        S.op("dve", lambda e: e.reciprocal(out=rsb.ap, in_=rsb.ap), reads=[rsb.res], writes=[rsb.res])
        for oc in range(4):
            S.op("dve", lambda e, oc=oc: e.scalar_tensor_tensor(out=cn.ap[:, oc, :], in0=c32.ap[:, oc, :],
                                                                scalar=gvec.ap[:, oc:oc + 1], in1=rsb.ap,
                                                                op0=ALU.mult, op1=ALU.mult),
                 reads=[c32.res, gvec.res, rsb.res], writes=[cn.res])

    def begin_phase(self, pring_n=6):
        self.A.reset()
        self.rings = {}
        self.ring_i = {}
        self.pring = self.banks[0:pring_n]
        self.pi = 0
        self.ti = 0

    def setup_norm(self, nx=2):
        self.mkring("xr", nx, [128, D], F32)
        self.mkring("stat", 2, [128, 4], F32)
        self.mkring("hb", 2, [128, D], BF16)
        self.junk = self.A.alloc([128, D], BF16, "junk")

    def latent(self, hT, wv, wres, gvec, c32, csq, rsb, cn):
        S = self.S
        for oc in range(4):
            b = self.bank()
            for k in range(16):
                self.mm(b.ap, b.res, wv[:, k, oc * 128:(oc + 1) * 128], hT.ap[:, k, :], [wres, hT.res], k == 0, k == 15)
            S.op("dve", lambda e, b=b, oc=oc: e.tensor_copy(out=c32.ap[:, oc, :], in_=b.ap), reads=[b.res],
                 writes=[c32.res])
            S.op("act", lambda e, oc=oc: e.activation(out=csq.ap[:, oc, :], in_=c32.ap[:, oc, :], func=AF.Square),
                 reads=[c32.res], writes=[csq.res])
        b = self.bank()
        for oc in range(4):
            self.mm(b.ap, b.res, self.ones.ap, csq.ap[:, oc, :], [self.ones.res, csq.res], oc == 0, oc == 3)
        S.op("act", lambda e: e.activation(out=rsb.ap, in_=b.ap, func=AF.Sqrt, bias=EPS, scale=1.0 / 512),
             reads=[b.res], writes=[rsb.res])
        S.op("dve", lambda e: e.reciprocal(out=rsb.ap, in_=rsb.ap), reads=[rsb.res], writes=[rsb.res])
        for oc in range(4):
            S.op("dve", lambda e, oc=oc: e.scalar_tensor_tensor(out=cn.ap[:, oc, :], in0=c32.ap[:, oc, :],
                                                                scalar=gvec.ap[:, oc:oc + 1], in1=rsb.ap,
                                                                op0=ALU.mult, op1=ALU.mult),
                 reads=[c32.res, gvec.res, rsb.res], writes=[cn.res])

    def rope(self, ba, bb, cst, sst, dst_ap, key_dst):
        S = self.S
        t1, _ = self.ring("rt")
        t2, _ = self.ring("rt")
        so, skey = self.ring("st")
        S.op("dve", lambda e: e.tensor_tensor(out=t1.ap, in0=ba.ap, in1=cst.ap, op=ALU.mult),
             reads=[ba.res, cst.res], writes=[t1.res])
        S.op("dve", lambda e: e.tensor_tensor(out=t2.ap, in0=bb.ap, in1=sst.ap, op=ALU.mult),
             reads=[bb.res, sst.res], writes=[t2.res])
        S.op("pool", lambda e: e.tensor_tensor(out=so.ap, in0=t1.ap, in1=t2.ap, op=ALU.add),
             reads=[t1.res, t2.res], writes=[so.res])
        self.store(dst_ap, so, skey)

    def phase_A1(self, J, first):
        sp = J["spec"]
        S, A, Z = self.S, self.A, self.Z
        self.begin_phase()
        self.setup_norm2()
        rw = self.r_w["A1"]

        def wres(name, shape, src):
            t = A.alloc(shape, BF16, name)
            self.load(t, src, "w_" + name, reads=[rw])
            return t
        wq = wres("wq", [128, 16, 512], Z["q"].rearrange("(c p) n -> p c n", p=128))
        wkv = wres("wkv", [128, 16, 512], Z["kv"].rearrange("(c p) n -> p c n", p=128))
        wkr = wres("wkr", [128, 16, 256], Z["kr"].rearrange("(c p) n -> p c n", p=128))
        wukvk = wres("wukvk", [128, 4, 1536], Z["ukvk"].rearrange("(c p) h d -> p c (h d)", p=128))
        wukvv = wres("wukvv", [128, 4, 1536], Z["ukvv"].rearrange("(c p) h d -> p c (h d)", p=128))
        wuqn = wres("wuqn", [128, 4, 1536], Z["uqn"].rearrange("(c p) h d -> p c (h d)", p=128))
        wuqr = wres("wuqr", [128, 4, 768], Z["uqr"].rearrange("(c p) h d -> p c (h d)", p=128))
        wuqrs = wres("wuqrs", [128, 4, 768], Z["uqrs"].rearrange("(c p) h d -> p c (h d)", p=128))
        gq = A.alloc([128, 4], F32, "gq")
        gkv = A.alloc([128, 4], F32, "gkv")
        self.load(gq, self.W["g_q_norm"].rearrange("o (c p) -> p (o c)", p=128), "gq", slow=True)
        self.load(gkv, self.W["g_kv_norm"].rearrange("o (c p) -> p (o c)", p=128), "gkv", slow=True)
        gb = A.alloc([128, D], F32, "gb")
        self.load(gb, self.W["g_attn"].partition_broadcast(128), "gb")
        hts = [A.alloc([128, 16, 512], BF16, "hT%d" % i) for i in range(2)]
        cst = A.alloc([128, 512], F32, "cst")
        sst = A.alloc([128, 512], F32, "sst")
        c32 = A.alloc([128, 4, 512], F32, "c32")
        csq = A.alloc([128, 4, 512], BF16, "csq")
        rsb = A.alloc([128, 512], F32, "rsb")
        cnkv = A.alloc([128, 4, 512], BF16, "cnkv")
        cnq = A.alloc([128, 4, 512], BF16, "cnq")
        self.mkring("st", 4, [128, 512], BF16)
        self.mkring("stv", 2, [128, 1536], BF16)
        self.mkring("rt", 3, [128, 512], F32)
        ntile = sp.S // 512
        self.norm_post(self.norm_pre(J["x"][0:512, :], gb), hts[0])
        for i in range(ntile):
            t0 = 512 * i
            isq = t0 < sp.nq
            hT = hts[i % 2]
            nxt = self.norm_pre(J["x"][t0 + 512:t0 + 1024, :], gb) if i + 1 < ntile else None
            self.load(cst, J["cs"][:, t0:t0 + 512], "cst")
            self.load(sst, J["ss"][:, t0:t0 + 512], "sst")
            self.latent_proj(hT, wkv.ap, wkv.res, c32, csq)
            ba = self.bank()
            for k in range(16):
                self.mm(ba.ap, ba.res, wkr.ap[:, k, 0:128], hT.ap[:, k, :], [wkr.res, hT.res], k == 0, k == 15)
            bb = self.bank()
            for k in range(16):
                self.mm(bb.ap, bb.res, wkr.ap[:, k, 128:256], hT.ap[:, k, :], [wkr.res, hT.res], k == 0, k == 15)
            self.rope(ba, bb, cst, sst, J["KrT"][:, t0:t0 + 512], None)
            self.latent_norm(gkv, c32, csq, rsb, cnkv)
            if nxt is not None:
                self.norm_post(nxt, hts[(i + 1) % 2])
            for h in range(H):
                b = self.bank()
                for k in range(4):
                    self.mm(b.ap, b.res, wukvk.ap[:, k, h * 128:(h + 1) * 128], cnkv.ap[:, k, :],
                            [wukvk.res, cnkv.res], k == 0, k == 3)
                so, skey = self.ring("st")
                self.evac(so.ap, so.res, b.ap, b.res)
                self.store(J["KnT"][h, :, t0:t0 + 512], so, skey)

            def emit_v(t0=t0):
                for j in range(4):
                    sv, vkey = self.ring("stv")
                    for vb in range(3):
                        b = self.bank()
                        for k in range(4):
                            self.mm(b.ap, b.res, cnkv.ap[:, k, j * 128:(j + 1) * 128], wukvv.ap[:, k, vb * 512:(vb + 1) * 512],
                                    [wukvv.res, cnkv.res], k == 0, k == 3)
                        self.evac(sv.ap[:, vb * 512:(vb + 1) * 512], sv.res, b.ap, b.res)
                    ch = (t0 + 128 * j) // 128
                    self.store(J["V"][:, :, ch, :].rearrange("h p d -> p h d"), sv, vkey,
                               src_ap=sv.ap.rearrange("p (h d) -> p h d", h=12))
            if isq:
                self.latent_proj(hT, wq.ap, wq.res, c32, csq)
                self.latent_norm(gq, c32, csq, rsb, cnq)
                emit_v()
                for h in range(H):
                    b = self.bank()
                    for k in range(4):
                        self.mm(b.ap, b.res, wuqn.ap[:, k, h * 128:(h + 1) * 128], cnq.ap[:, k, :],
                                [wuqn.res, cnq.res], k == 0, k == 3)
                    so, skey = self.ring("st")
                    self.evac(so.ap, so.res, b.ap, b.res)
                    self.store(J["QnT"][h, :, t0:t0 + 512], so, skey)
                for pr in range(6):
                    ba = self.bank()
                    for k in range(4):
                        self.mm(ba.ap, ba.res, wuqr.ap[:, k, pr * 128:(pr + 1) * 128], cnq.ap[:, k, :],
                                [wuqr.res, cnq.res], k == 0, k == 3)
                    bb = self.bank()
                    for k in range(4):
                        self.mm(bb.ap, bb.res, wuqrs.ap[:, k, pr * 128:(pr + 1) * 128], cnq.ap[:, k, :],
                                [wuqrs.res, cnq.res], k == 0, k == 3)
                    self.rope(ba, bb, cst, sst, J["QrT"][pr, :, t0:t0 + 512], None)
            else:
                emit_v()
        S.barrier()

    def phase_A2(self, J, first):
        sp = J["spec"]
        S, A, Z = self.S, self.A, self.Z
        self.begin_phase()
        self.setup_norm2()
        rw = self.r_w["A2"]
        gb = A.alloc([128, D], F32, "gb")
        self.load(gb, self.W["g_attn"].partition_broadcast(128), "gb")
        hts = [A.alloc([128, 16, 512], BF16, "hT%d" % i) for i in range(2)]
        self.mkring("wr", 4, [128, 8192], BF16)
        self.mkring("st", 4, [128, 512], BF16)
        self.mkring("stv", 8, [128, 1536], BF16)
        zdil = Z["dil"].rearrange("(c p) n -> p c n", p=128)
        tiles = [("Q", J["x"][512 * i:512 * i + 512, :], PAD + 512 * i, 512 * i) for i in range(sp.nq // 512)]
        if sp.nh:
            for k in range(sp.nh // 512):
                po = 512 * k if k < 2 else sp.nq + PAD + 512 * (k - 2)
                tiles.append(("H", J["xh"][512 * k:512 * k + 512, :], po, None))
        else:
            z = A.alloc([128, PAD], BF16, "zero")
            S.op("dve", lambda e: e.memset(z.ap, 0.0), writes=[z.res])
            for gh in range(12):
                for po in (0, PAD + sp.nq):
                    self.store(J["dKT"][gh, :, po:po + PAD], z, "zst")
            for po in (0, PAD + sp.nq):
                for rb in range(PAD // 128):
                    self.store(J["dV"][po + rb * 128:po + rb * 128 + 128, 0:1024], z, "zst", src_ap=z.ap[:, 0:1024])
                    self.store(J["dV"][po + rb * 128:po + rb * 128 + 128, 1024:1536], z, "zst", src_ap=z.ap[:, 0:512])
        self.norm_post(self.norm_pre(tiles[0][1], gb), hts[0])
        for ti, (kind, src, po, qo) in enumerate(tiles):
            hT = hts[ti % 2]
            nxt = self.norm_pre(tiles[ti + 1][1], gb) if ti + 1 < len(tiles) else None
            svs = [self.ring("stv") for _ in range(4)]
            for blk in range(9):
                which, g = blk // 3, blk % 3
                if blk == 6 and nxt is not None:
                    self.norm_post(nxt, hts[(ti + 1) % 2])
                if kind == "H" and which == 0:
                    continue
                col0 = which * 1536 + g * 512
                wv, wr = self.wblock(zdil[:, :, col0:col0 + 512], 16, 512, [rw])

                if which < 2:
                    for c in range(4):
                        b = self.bank()
                        for k in range(16):
                            self.mm(b.ap, b.res, wv[:, k, c * 128:(c + 1) * 128], hT.ap[:, k, :], [wr, hT.res],
                                    k == 0, k == 15)
                        so, skey = self.ring("st")
                        self.evac(so.ap, so.res, b.ap, b.res)
                        gh = g * 4 + c
                        if which == 0:
                            self.store(J["dQT"][gh, :, qo:qo + 512], so, skey)
                        else:
                            self.store(J["dKT"][gh, :, po:po + 512], so, skey)
                else:
                    for j in range(4):
                        b = self.bank()
                        for k in range(16):
                            self.mm(b.ap, b.res, hT.ap[:, k, j * 128:(j + 1) * 128], wv[:, k, :], [wr, hT.res],
                                    k == 0, k == 15)
                        sv, vkey = svs[j]
                        self.evac(sv.ap[:, g * 512:(g + 1) * 512], sv.res, b.ap, b.res)
            for j in range(4):
                sv, vkey = svs[j]
                self.store(J["dV"][po + 128 * j:po + 128 * j + 128, :], sv, vkey)
        S.barrier()

    def phase_M(self, J, first):
        sp = J["spec"]
        S, A = self.S, self.A
        self.begin_phase(5)
        nkc = sp.S // 128
        krA = A.alloc([128, sp.S], BF16, "krA")
        krB = A.alloc([128, sp.S], BF16, "krB")
        S.op("dve", lambda e: e.memset(krA.ap[64:128, :], 0.0), writes=[krA.res])
        S.op("dve", lambda e: e.memset(krB.ap[0:64, :], 0.0), writes=[krB.res])
        self.load(krA, J["KrT"][0:64, :], "krA", dst_ap=krA.ap[0:64, :])
        self.load(krB, J["KrT"][64:128, :], "krB", dst_ap=krB.ap[64:128, :])
        self.mkring("kn", 2, [128, sp.S], BF16)
        self.mkring("vh", 2, [128, nkc, 128], BF16)
        self.mkring("qn", 2, [128, 512], BF16)
        self.mkring("qr", 2, [128, 512], BF16)
        self.mkring("pt", 8, [128, 512], BF16)
        self.mkring("ps", 3, [128, 512], BF16)
        self.mkring("rc", 2, [128, 512], F32)
        self.mkring("st", 2, [128, 512], BF16)
        accs = [(self.banks[5], self.banks[7]), (self.banks[6], self.banks[7])]
        accsb = [(A.alloc([128, 512], F32, "accD%d" % i), A.alloc([128, 512], F32, "accP%d" % i)) for i in range(2)]
        ones32 = A.alloc([128, 128], F32, "ones32")
        S.op("dve", lambda e: e.memset(ones32.ap, 1.0), writes=[ones32.res])
        ai = 0
        for h in range(H):
            kn, kkey = self.ring("kn")
            self.load(kn, J["KnT"][h], kkey)
            vh, vkey = self.ring("vh")
            self.load(vh, J["V"][h], vkey)
            hp = h % 2
            for qt in range(sp.nq // 512):
                q0 = qt * 512
                qn, qkey = self.ring("qn")
                self.load(qn, J["QnT"][h, :, q0:q0 + 512], qkey)
                qr, rkey = self.ring("qr")
                self.load(qr, J["QrT"][h // 2, :, q0:q0 + 512], rkey)
                bo, bs = accs[ai % 2]
                accD, accP = accsb[ai % 2]
                ai += 1
                LA = 3
                pts = {}
                for step in range(nkc + LA):
                    if step < nkc:
                        kc = step
                        b = self.bank()
                        self.mm(b.ap, b.res, kn.ap[:, kc * 128:(kc + 1) * 128], qn.ap, [kn.res, qn.res], True, False)
                        kr_ = krA if hp == 0 else krB
                        self.mm(b.ap, b.res, kr_.ap[:, kc * 128:(kc + 1) * 128], qr.ap, [kr_.res, qr.res], False, True)
                        pt, _ = self.ring("pt")
                        S.op("act", lambda e, b=b, pt=pt: e.activation(out=pt.ap, in_=b.ap, func=AF.Exp, scale=SC_MLA),
                             reads=[b.res], writes=[pt.res])
                        pts[kc] = pt
                    kc = step - LA
                    if kc >= 0:
                        pt = pts.pop(kc)
                        self.mm(bo.ap, bo.res, vh.ap[:, kc, :], pt.ap, [vh.res, pt.res], kc == 0, kc == nkc - 1)
                        if kc % 2 == 0:
                            ptprev = pt
                        else:
                            ps_, _ = self.ring("ps")
                            S.op("dve", lambda e, ps_=ps_, a=ptprev, b2=pt: e.tensor_tensor(out=ps_.ap, in0=a.ap, in1=b2.ap,
                                                                                          op=ALU.add),
                                 reads=[ptprev.res, pt.res], writes=[ps_.res])
                            if kc == 1:
                                S.op("dve", lambda e, ps_=ps_, ac=accD: e.tensor_copy(out=ac.ap, in_=ps_.ap),
                                     reads=[ps_.res], writes=[accD.res])
                            else:
                                S.op("dve", lambda e, ps_=ps_, ac=accD: e.tensor_tensor(out=ac.ap, in0=ps_.ap, in1=ac.ap,
                                                                                      op=ALU.add),
                                     reads=[ps_.res, accD.res], writes=[accD.res])
                self.mm(bs.ap, bs.res, ones32.ap, accD.ap, [ones32.res, accD.res], True, True)
                rc, _ = self.ring("rc")
                S.op("dve", lambda e, rc=rc, bs=bs: e.reciprocal(out=rc.ap, in_=bs.ap), reads=[bs.res], writes=[rc.res])
                so, skey = self.ring("st")
                S.op("dve", lambda e, so=so, bo=bo, rc=rc: e.tensor_tensor(out=so.ap, in0=bo.ap, in1=rc.ap, op=ALU.mult),
                     reads=[bo.res, rc.res], writes=[so.res])
                self.store(J["OmT"][h, :, q0:q0 + 512], so, skey)
                if first:
                    self.drip(2)
        S.barrier()

    def phase_T(self):
        S, A, W = self.S, self.A, self.W
        self.begin_phase()
        rb = A.alloc([33, 12], F32, "rb")
        S.op("dve", lambda e: e.memset(rb.ap, MASKV), writes=[rb.res])
        self.load(rb, W["rel_bias"], "rb", dst_ap=rb.ap[0:32, :])
        S.op("dve", lambda e: e.tensor_scalar(out=rb.ap, in0=rb.ap, scalar1=float(1.0 / SC_DIL), scalar2=None,
                                              op0=ALU.mult), reads=[rb.res], writes=[rb.res])
        oh = A.alloc([33, 3, 384], F32, "oh")
        self.load(oh, W["oh"].rearrange("g b i -> b g i"), "oh")
        aid = A.alloc([128, 128], F32, "aid")
        self.load(aid, W["antiid"], "aid")
        vps = A.alloc([4, 3, 384], F32, "vps")
        for g in range(3):
            b = self.bank()
            self.mm(b.ap[0:4, 0:384], b.res, rb.ap[:, g * 4:(g + 1) * 4], oh.ap[:, g, :], [rb.res, oh.res], True, True)
            S.op("dve", lambda e, b=b, g=g: e.tensor_copy(out=vps.ap[:, g, :], in_=b.ap[0:4, 0:384]), reads=[b.res],
                 writes=[vps.res])
        r_vp = Res("vp", multi=True)
        self.store(self.vp.rearrange("(g h) i -> h g i", g=3), vps, "vps", writes=[r_vp])
        self.mkring("hk", 2, [128, 128], F32)
        self.mkring("t32", 2, [128, 128], F32)
        thl = A.alloc([128, 2, 24, 128], BF16, "thl")
        self.mkring("tt", 2, [128, 128], F32)
        for gh in range(12):
            for j in range(2):
                hk, hkey = self.ring("hk")
                src = bass.AP(self.vp.tensor, gh * 384 + 128 * j, [[1, 128], [1, 128]])
                self.load(hk, src, hkey, reads=[r_vp])
                b = self.bank()
                self.mm(b.ap[:, 0:128], b.res, hk.ap, aid.ap, [hk.res, aid.res], True, True)
                t32, _ = self.ring("t32")
                S.op("dve", lambda e, b=b, t32=t32: e.tensor_copy(out=t32.ap, in_=b.ap[:, 0:128]), reads=[b.res],
                     writes=[t32.res])
                idx = gh * 2 + j
                S.op("dve", lambda e, t32=t32, idx=idx: e.tensor_copy(out=thl.ap[:, 0, idx, :], in_=t32.ap),
                     reads=[t32.res], writes=[thl.res])
                tt, _ = self.ring("tt")
                S.op("dve", lambda e, t32=t32, idx=idx, tt=tt: e.tensor_tensor(out=tt.ap, in0=t32.ap,
                                                                            in1=thl.ap[:, 0, idx, :], op=ALU.subtract),
                     reads=[t32.res, thl.res], writes=[tt.res])
                S.op("dve", lambda e, tt=tt, idx=idx: e.tensor_copy(out=thl.ap[:, 1, idx, :], in_=tt.ap),
                     reads=[tt.res], writes=[thl.res])
        self.store(self.Tsc, thl, "thl")
        S.barrier()

    def phase_D(self, J):
        sp = J["spec"]
        S, A = self.S, self.A
        self.begin_phase(6)
        nq = sp.nq
        thl = A.alloc([128, 2, 24, 128], BF16, "thl")
        self.load(thl, self.Tsc, "thl")
        ncol = vb_cols(nq)
        vbt = A.alloc([128, sum(ncol)], F32, "vbt")
        self.load(vbt, J["vb"], "vbt")
        acc = A.alloc([128, 2, nq], F32, "acc")
        self.mkring("dq", 2, [128, nq], BF16)
        self.mkring("dk", 2, [128, nq + 2 * PAD], BF16)
        self.mkring("vt", 10, [128, 128], BF16)
        self.mkring("pt", 5, [128, 2, 128], BF16)
        self.mkring("st", 2, [128, nq], BF16)
        rcp = A.alloc([128, nq], F32, "rcp")
        for hh in range(4):
            for g, r in enumerate(DIL_R):
                gh = g * 4 + hh
                dq, qkey = self.ring("dq")
                self.load(dq, J["dQT"][gh], qkey)
                dk, kkey = self.ring("dk")
                self.load(dk, J["dKT"][gh], kkey)
                nb = nq // (128 * r)
                colbase = sum(ncol[:g])
                blocks = [(c, bq) for c in range(r) for bq in range(nb)]
                vts = {}

                def get_vt(c, m, r=r, gh=gh, vts=vts):
                    if (c, m) not in vts:
                        vt, vkey = self.ring("vt")
                        row0 = PAD + c - 64 * r + 128 * r * m
                        src = bass.AP(J["dV"].tensor, row0 * 1536 + gh * 128, [[r * 1536, 128], [1, 128]])
                        self.load(vt, src, vkey)
                        vts[(c, m)] = vt
                    return vts[(c, m)]
                LA = 2
                pend = {}
                for step in range(len(blocks) + LA):
                    if step < len(blocks):
                        c, bq = blocks[step]
                        vt0 = get_vt(c, bq)
                        vt1 = get_vt(c, bq + 1)
                        qs = c + r * 128 * bq
                        q_ap = dq.ap[:, qs:qs + 127 * r + 1:r]
                        b = self.bank()
                        b3 = b.ap.rearrange("p (a t) -> p a t", a=4)
                        for j in range(2):
                            ks = PAD + c - 64 * r + 128 * r * (bq + j)
                            k_ap = dk.ap[:, ks:ks + 127 * r + 1:r]
                            self.mm(b3[:, j, :], b.res, k_ap, q_ap, [dk.res, dq.res], True, False)
                            self.mm(b3[:, j, :], b.res, self.ident.ap, thl.ap[:, 0, gh * 2 + j, :],
                                    [self.ident.res, thl.res], False, False)
                            self.mm(b3[:, j, :], b.res, self.ident.ap, thl.ap[:, 1, gh * 2 + j, :],
                                    [self.ident.res, thl.res], False, True)
                        pt, _ = self.ring("pt")
                        for j in range(2):
                            col = colbase + c * (nb + 1) + bq + j
                            S.op("act", lambda e, b3=b3, pt=pt, j=j, col=col: e.activation(
                                out=pt.ap[:, j, :], in_=b3[:, j, :], func=AF.Exp, scale=SC_DIL,
                                bias=vbt.ap[:, col:col + 1]), reads=[b.res, vbt.res], writes=[pt.res])
                        pend[step] = (qs, pt, vt0, vt1)
                    k = step - LA
                    if k >= 0:
                        qs, pt, vt0, vt1 = pend.pop(k)
                        bo = self.bank()
                        o3 = bo.ap.rearrange("p (a t) -> p a t", a=4)
                        for j, vt in enumerate((vt0, vt1)):
                            self.mm(o3[:, 0, :], bo.res, vt.ap, pt.ap[:, j, :], [vt.res, pt.res], j == 0, j == 1)
                        for j in range(2):
                            self.mm(o3[:, 1, :], bo.res, self.ones.ap, pt.ap[:, j, :], [self.ones.res, pt.res],
                                    j == 0, j == 1)
                        a_ap = acc.ap[:, :, qs:qs + 127 * r + 1:r]
                        if g == 0:
                            S.op("dve", lambda e, a_ap=a_ap, o3=o3: e.tensor_copy(out=a_ap, in_=o3[:, 0:2, :]),
                                 reads=[bo.res], writes=[acc.res])
                        else:
                            S.op("dve", lambda e, a_ap=a_ap, o3=o3: e.tensor_tensor(out=a_ap, in0=o3[:, 0:2, :],
                                                                                 in1=a_ap, op=ALU.add),
                                 reads=[bo.res, acc.res], writes=[acc.res])
            S.op("dve", lambda e: e.reciprocal(out=rcp.ap, in_=acc.ap[:, 1, :]), reads=[acc.res], writes=[rcp.res])
            so, skey = self.ring("st")
            S.op("dve", lambda e, so=so: e.tensor_tensor(out=so.ap, in0=acc.ap[:, 0, :], in1=rcp.ap, op=ALU.mult),
                 reads=[acc.res, rcp.res], writes=[so.res])
            self.store(J["OdT"][hh], so, skey)
        S.barrier()

    def phase_B(self, J):
        sp = J["spec"]
        S, A, Z, W = self.S, self.A, self.Z, self.W
        self.begin_phase(6)
        rw = self.r_w["B"]
        self.mkring("stat", 4, [128, 4], F32)
        self.mkring("hb", 4, [128, D], BF16)
        self.junk = A.alloc([128, D], BF16, "junk")
        gb = A.alloc([128, D], F32, "gb")
        hT = A.alloc([128, 16, 512], BF16, "hT")
        self.mkring("wr", 3, [128, 8192], BF16)
        xres = [A.alloc([128, D], F32, "xres%d" % j) for j in range(4)]
        kmT = A.alloc([128, 8, 256], BF16, "kmT")
        vm = A.alloc([128, 2, 1024], BF16, "vm")
        self.load(gb, W["g_mem"].partition_broadcast(128), "gb")
        for j in range(2):
            self.load(xres[j], J["mem"][j * 128:(j + 1) * 128, :], "xres%d" % j)
            self.norm_T_sub(xres[j], gb, hT, j)
        zm = Z["mkv"].rearrange("(c p) n -> p c n", p=128)
        for blk in range(4):
            wv, wr = self.wblock(zm[:, :, blk * 512:(blk + 1) * 512], 16, 512, [rw])
            if blk < 2:
                for c in range(4):
                    b = self.bank()
                    for k in range(16):
                        self.mm(b.ap[:, 0:256], b.res, wv[:, k, c * 128:(c + 1) * 128], hT.ap[:, k, 0:256], [wr, hT.res],
                                k == 0, k == 15)
                    self.evac(kmT.ap[:, blk * 4 + c, :], kmT.res, b.ap[:, 0:256], b.res)
            else:
                for mc in range(2):
                    b = self.bank()
                    for k in range(16):
                        self.mm(b.ap, b.res, hT.ap[:, k, mc * 128:(mc + 1) * 128], wv[:, k, :], [wr, hT.res],
                                k == 0, k == 15)
                    self.evac(vm.ap[:, mc, (blk - 2) * 512:(blk - 1) * 512], vm.res, b.ap, b.res)
        big = A.alloc([128, 32, 512], BF16, "big")
        obr = Tile(big.ap[:, 0:24, :], Res("obr"))
        xq = Tile(big.ap[:, 24:32, :], Res("xq"))
        u = big
        ures = [obr.res, xq.res]
        mrg = A.alloc([128, 16, 512], BF16, "mrg")
        self.mkring("pt", 3, [128, 2, 512], BF16)
        self.mkring("rc", 2, [128, 512], F32)
        self.mkring("gt", 3, [128, 512], F32)
        self.mkring("mt", 3, [128, 512], F32)
        zxq = Z["xq"].rearrange("(c p) n -> p c n", p=128)
        zg = Z["g"].rearrange("(c p) fc br d -> p c fc (br d)", p=128)
        zb = Z["b"].rearrange("(c p) n -> p c n", p=128)
        zo = Z["out"].rearrange("(c p) n -> p c n", p=128)
        zu = Z["up"].rearrange("(c p) n -> p c n", p=128)
        zd = Z["down"].rearrange("(c p) n -> p c n", p=128)
        def npre(xt):
            st, _ = self.ring("stat")
            hb, _ = self.ring("hb")
            S.op("act", lambda e: e.activation(out=hb.ap, in_=xt.ap, func=AF.Square, scale=float(D ** -0.5),
                                               accum_out=st.ap[:, 0:1]), reads=[xt.res], writes=[hb.res, st.res])
            S.op("act", lambda e: e.activation(out=st.ap[:, 1:2], in_=st.ap[:, 0:1], func=AF.Sqrt, bias=EPS),
                 reads=[st.res], writes=[st.res])
            S.op("dve", lambda e: e.reciprocal(out=st.ap[:, 2:3], in_=st.ap[:, 1:2]), reads=[st.res], writes=[st.res])
            S.op("dve", lambda e: e.scalar_tensor_tensor(out=hb.ap, in0=xt.ap, scalar=st.ap[:, 2:3], in1=gb.ap,
                                                         op0=ALU.mult, op1=ALU.mult),
                 reads=[xt.res, st.res, gb.res], writes=[hb.res])
            return hb

        for i in range(sp.nq // 512):
            t0 = 512 * i
            self.load(gb, W["g_attn"].partition_broadcast(128), "gb")
            hbs = []
            for j in range(4):
                self.load(xres[j], J["x"][t0 + 128 * j:t0 + 128 * j + 128, :], "xres%d" % j)
                hbs.append(npre(xres[j]))
            self.norm_post(hbs, hT)
            for blk in range(2):
                wv, wr = self.wblock(zxq[:, :, blk * 512:(blk + 1) * 512], 16, 512, [rw])
                for c in range(4):
                    b = self.bank()
                    for k in range(16):
                        self.mm(b.ap, b.res, wv[:, k, c * 128:(c + 1) * 128], hT.ap[:, k, :], [wr, hT.res], k == 0, k == 15)
                    self.evac(xq.ap[:, blk * 4 + c, :], xq.res, b.ap, b.res)
            self.load(obr, J["OmT"][:, :, t0:t0 + 512].rearrange("h p t -> p h t"), "obr_m", dst_ap=obr.ap[:, 0:12, :])
            self.load(obr, J["OdT"][:, :, t0:t0 + 512].rearrange("h p t -> p h t"), "obr_d", dst_ap=obr.ap[:, 12:16, :])
            def mem_scores(hm):
                pt, _ = self.ring("pt")
                for mc in range(2):
                    bq_ = self.bank()
                    for dc in range(2):
                        self.mm(bq_.ap, bq_.res, kmT.ap[:, hm * 2 + dc, mc * 128:(mc + 1) * 128], xq.ap[:, hm * 2 + dc, :],
                                [kmT.res, xq.res], dc == 0, dc == 1)
                    S.op("act", lambda e, bq_=bq_, pt=pt, mc=mc: e.activation(out=pt.ap[:, mc, :], in_=bq_.ap, func=AF.Exp,
                                                                            scale=SC_MEM), reads=[bq_.res], writes=[pt.res])
                return pt

            def mem_pv(hm, pt):
                bs = self.bank()
                for mc in range(2):
                    self.mm(bs.ap, bs.res, self.ones.ap, pt.ap[:, mc, :], [self.ones.res, pt.res], mc == 0, mc == 1)
                rc, _ = self.ring("rc")
                S.op("dve", lambda e: e.reciprocal(out=rc.ap, in_=bs.ap), reads=[bs.res], writes=[rc.res])
                for dvc in range(2):
                    bv = self.bank()
                    for mc in range(2):
                        c0 = hm * 256 + dvc * 128
                        self.mm(bv.ap, bv.res, vm.ap[:, mc, c0:c0 + 128], pt.ap[:, mc, :], [vm.res, pt.res], mc == 0, mc == 1)
                    S.op("dve", lambda e, bv=bv, dvc=dvc: e.tensor_tensor(
                        out=obr.ap[:, 16 + hm * 2 + dvc, :], in0=bv.ap, in1=rc.ap, op=ALU.mult),
                        reads=[bv.res, rc.res], writes=[obr.res])
            pts_ = {0: mem_scores(0)}
            for hm in range(4):
                if hm + 1 < 4:
                    pts_[hm + 1] = mem_scores(hm + 1)
                mem_pv(hm, pts_.pop(hm))
            for fc in range(16):
                wg, wgr = self.wblock(zg[:, :, fc, :], 16, 384, [rw])
                if fc % 2 == 0:
                    wb, wbr = self.wblock(zb[:, :, fc * 128:(fc + 2) * 128], 24, 256, [rw])
                gts = []
                for br in range(3):
                    b = self.bank()
                    for k in range(16):
                        self.mm(b.ap, b.res, wg[:, k, br * 128:(br + 1) * 128], hT.ap[:, k, :], [wgr, hT.res], k == 0, k == 15)
                    gt, _ = self.ring("gt")
                    S.op("act", lambda e, b=b, gt=gt: e.activation(out=gt.ap, in_=b.ap, func=AF.Sigmoid),
                         reads=[b.res], writes=[gt.res])
                    gts.append(gt)
                mts = []
                for br, (k0, k1) in enumerate(((0, 12), (12, 16), (16, 24))):
                    b = self.bank()
                    for k in range(k0, k1):
                        self.mm(b.ap, b.res, wb[:, k, (fc % 2) * 128:(fc % 2 + 1) * 128], obr.ap[:, k, :], [wbr, obr.res],
                                k == k0, k == k1 - 1)
                    mt, _ = self.ring("mt")
                    S.op("dve", lambda e, b=b, mt=mt, gt=gts[br]: e.tensor_tensor(out=mt.ap, in0=b.ap, in1=gt.ap,
                                                                                op=ALU.mult),
                         reads=[b.res, gts[br].res], writes=[mt.res])
                    mts.append(mt)
                S.op("pool", lambda e, a=mts[0], b2=mts[1]: e.tensor_tensor(out=a.ap, in0=a.ap, in1=b2.ap, op=ALU.add),
                     reads=[mts[0].res, mts[1].res], writes=[mts[0].res])
                S.op("pool", lambda e, a=mts[0], b2=mts[2], fc=fc: e.tensor_tensor(out=mrg.ap[:, fc, :], in0=a.ap,
                                                                                in1=b2.ap, op=ALU.add),
                     reads=[mts[0].res, mts[2].res], writes=[mrg.res])
            hbs2 = []
            for fb in range(4):
                wv, wr = self.wblock(zo[:, :, fb * 512:(fb + 1) * 512], 16, 512, [rw])
                for j in range(4):
                    b = self.bank()
                    for k in range(16):
                        self.mm(b.ap, b.res, mrg.ap[:, k, j * 128:(j + 1) * 128], wv[:, k, :], [wr, mrg.res], k == 0, k == 15)
                    xs = xres[j].ap[:, fb * 512:(fb + 1) * 512]
                    S.op("dve", lambda e, b=b, xs=xs: e.tensor_tensor(out=xs, in0=b.ap, in1=xs, op=ALU.add),
                         reads=[b.res, xres[j].res], writes=[xres[j].res])
                    if fb == 3:
                        if j == 0:
                            self.load(gb, W["g_mlp"].partition_broadcast(128), "gb")
                        hbs2.append(npre(xres[j]))
            self.norm_post(hbs2, hT)
            for half in range(2):
                for fb in range(8):
                    col0 = half * 4096 + fb * 512
                    wv, wr = self.wblock(zu[:, :, col0:col0 + 512], 16, 512, [rw])
                    for c in range(4):
                        b = self.bank()
                        for k in range(16):
                            self.mm(b.ap, b.res, wv[:, k, c * 128:(c + 1) * 128], hT.ap[:, k, :], [wr, hT.res],
                                    k == 0, k == 15)
                        rl, _ = self.ring("gt")
                        S.op("act", lambda e, b=b, rl=rl: e.activation(out=rl.ap, in_=b.ap, func=AF.Relu),
                             reads=[b.res], writes=[rl.res])
                        S.op("dve", lambda e, rl=rl, fb=fb, c=c: e.tensor_tensor(out=u.ap[:, fb * 4 + c, :], in0=rl.ap,
                                                                                in1=rl.ap, op=ALU.mult),
                             reads=[rl.res], writes=ures)
                for fb in range(4):
                    wbs = []
                    for kb in range(2):
                        r0 = half * 32 + kb * 16
                        wbs.append(self.wblock(zd[:, r0:r0 + 16, fb * 512:(fb + 1) * 512], 16, 512, [rw]))
                    last = half == 1 and fb == 3
                    if last:
                        self.load(gb, W["g_final"].partition_broadcast(128), "gb")
                    for j in range(4):
                        bk = self.bank()
                        for kk in range(32):
                            wv, wr = wbs[kk // 16]
                            self.mm(bk.ap, bk.res, u.ap[:, kk, j * 128:(j + 1) * 128], wv[:, kk % 16, :], [wr, *ures],
                                    kk == 0, kk == 31)
                        xt = xres[j]
                        xs = xt.ap[:, fb * 512:(fb + 1) * 512]
                        S.op("dve", lambda e, bk=bk, xs=xs: e.tensor_tensor(out=xs, in0=bk.ap, in1=xs, op=ALU.add),
                             reads=[bk.res, xt.res], writes=[xt.res])
                        if last:
                            st, _ = self.ring("stat")
                            junk = self.junk
                            S.op("act", lambda e, st=st, xt=xt, junk=junk: e.activation(
                                out=junk.ap, in_=xt.ap, func=AF.Square, scale=float(D ** -0.5), accum_out=st.ap[:, 0:1]),
                                reads=[xt.res], writes=[junk.res, st.res])
                            S.op("act", lambda e, st=st: e.activation(out=st.ap[:, 1:2], in_=st.ap[:, 0:1], func=AF.Sqrt,
                                                                     bias=EPS), reads=[st.res], writes=[st.res])
                            S.op("dve", lambda e, st=st: e.reciprocal(out=st.ap[:, 2:3], in_=st.ap[:, 1:2]),
                                 reads=[st.res], writes=[st.res])
                            S.op("dve", lambda e, st=st, xt=xt: e.scalar_tensor_tensor(
                                out=xt.ap, in0=xt.ap, scalar=st.ap[:, 2:3], in1=gb.ap, op0=ALU.mult, op1=ALU.mult),
                                reads=[xt.res, st.res, gb.res], writes=[xt.res])
                            self.store(J["y"][t0 + 128 * j:t0 + 128 * j + 128, :], xt, "ys%d" % j)
        S.barrier()

    def build(self, phases=("T", "A1", "A2", "M", "D", "B")):
        nc = self.nc
        self.declare()
        with contextlib.ExitStack() as es:
            arena = es.enter_context(nc.sbuf_tensor("arena", [128, ARENA_ELEMS], BF16))
            ps = es.enter_context(nc.psum_tensor("ps", [128, 8, 512], F32))
            self.A = Arena(arena, ARENA_ELEMS)
            self.banks = [Tile(ps[:, i, :], Res("bank%d" % i)) for i in range(8)]
            self.tbanks = [Tile(ps[:, 6 + i, :].bitcast(BF16).rearrange("p (c t) -> p c t", c=8), self.banks[6 + i].res)
                           for i in range(2)]
            self.rings = {}
            self.ring_i = {}
            self.ident = self.A.alloc([128, 128], BF16, "ident")
            self.ones = self.A.alloc([128, 128], BF16, "ones")
            self.load(self.ident, self.W["ident"], "ident")
            self.S.op("dve", lambda e: e.memset(self.ones.ap, 1.0), writes=[self.ones.res])
            self.A.set_mark()
            self.phase_W_early()
            if "T" in phases:
                self.phase_T()
            for ji, J in enumerate(self.jobs):
                first = ji == 0
                if "A1" in phases:
                    self.phase_A1(J, first)
                if "A2" in phases:
                    self.phase_A2(J, first)
                if "M" in phases:
                    self.phase_M(J, first)
                if first:
                    self.drip(1000)
                if "D" in phases:
                    self.phase_D(J)
                if "B" in phases:
                    self.phase_B(J)
            cnt = self.S.emit()
            self.cnt = cnt
        return nc


def common_inputs(inp):
    d = {}
    for k in ("w_in", "w_uq", "w_ukv", "w_mem_kv", "w_b_mla", "w_b_dil", "w_b_mem", "w_out", "w_up", "w_down"):
        d[k] = np.ascontiguousarray(np.asarray(inp[k], dtype=np.float32)[0])
    for k in ("g_attn", "g_mem", "g_mlp", "g_q_norm", "g_kv_norm"):
        d[k] = np.ascontiguousarray(np.asarray(inp[k], dtype=np.float32).reshape(1, -1))
    d["g_final"] = np.ascontiguousarray(np.asarray(inp["g_final"], dtype=np.float32).reshape(1, -1))
    d["rel_bias"] = np.ascontiguousarray(np.asarray(inp["rel_bias"], dtype=np.float32))
    d["oh"] = onehot_tables()
    d["antiid"] = np.ascontiguousarray(np.eye(128, dtype=np.float32)[::-1])
    d["ident"] = np.eye(128).astype(ml_dtypes.bfloat16)
    return d


_CACHE = {}


def kernel(**inputs):
    n = 8
    specs = [JobSpec("S", 4096, 4096, 0), JobSpec("P", 8192, 2048, 2 * PAD)]
    if "nc" not in _CACHE:
        _CACHE["nc"] = Builder(specs).build()
    nc = _CACHE["nc"]
    com = common_inputs(inputs)
    xs = np.asarray(inputs["x_sample"], dtype=np.float32)
    xp = np.asarray(inputs["x_prompt"], dtype=np.float32)
    ms = np.asarray(inputs["mem_sample"], dtype=np.float32)
    mp = np.asarray(inputs["mem_prompt"], dtype=np.float32)
    in_maps = []
    for c in range(n):
        d = dict(com)
        d.update(prep_job_inputs(specs[0], xs[c], ms[c], 0))
        d.update(prep_job_inputs(specs[1], xp[c // 4], mp[c // 4], (c % 4) * 2048))
        in_maps.append(d)
    res = run_bass_kernel_spmd(nc, in_maps, core_ids=list(range(n)))
    y_s = np.stack([np.asarray(res.results[c]["y_S"], dtype=np.float32) for c in range(n)], axis=0)
    y_p = np.zeros((2, 8192, D), np.float32)
    for c in range(n):
        y_p[c // 4, (c % 4) * 2048:(c % 4 + 1) * 2048] = np.asarray(res.results[c]["y_P"], dtype=np.float32)
    return (y_p, y_s)
```

```python
import contextlib
import math
import numpy as np
import ml_dtypes
import concourse.bass as bass
import concourse.mybir as mybir
from concourse.bass_utils import run_bass_kernel_spmd

F32 = mybir.dt.float32
BF16 = mybir.dt.bfloat16
AF = mybir.ActivationFunctionType
ALU = mybir.AluOpType

D = 2048
H = 12
NBUCK = 32
EPS = 1e-6
DIL_R = (1, 4, 16)
PAD = 1024
MASKV = -30000.0
SC_MLA = float(192 ** -0.5)
SC_DIL = float(128 ** -0.5)
SC_MEM = float(256 ** -0.5)
ARENA_ELEMS = 104400


class Res:
    __slots__ = ("name", "last_w", "readers", "multi", "writers")

    def __init__(self, name, multi=False):
        self.name = name
        self.last_w = None
        self.readers = []
        self.multi = multi
        self.writers = []


class Op:
    __slots__ = ("eng", "fn", "reads", "writes", "dma_key", "group", "deps", "signal", "sem", "ticket")

    def __init__(self, eng, fn, reads, writes, dma_key, group):
        self.eng = eng
        self.fn = fn
        self.reads = reads
        self.writes = writes
        self.dma_key = dma_key
        self.group = group
        self.deps = []
        self.signal = dma_key is not None
        self.sem = None
        self.ticket = 0


class Sched:
    ENGS = ("pe", "act", "dve", "pool", "sp")

    def __init__(self, nc):
        self.nc = nc
        self.ops = []
        self.last_eng = {e: None for e in self.ENGS}
        self.last_dma = {}

    def op(self, eng, fn, reads=(), writes=(), dma_key=None, group=False):
        o = Op(eng, fn, tuple(reads), tuple(writes), dma_key, group)
        deps = {}
        raw = set()
        for r in o.reads:
            if r.multi:
                for p in r.writers:
                    deps[id(p)] = p
                    raw.add(id(p))
            elif r.last_w is not None:
                deps[id(r.last_w)] = r.last_w
                raw.add(id(r.last_w))
        for w in o.writes:
            if w.multi:
                w.writers.append(o)
                continue
            if w.last_w is not None:
                deps[id(w.last_w)] = w.last_w
            for p in w.readers:
                deps[id(p)] = p
            w.last_w = o
            w.readers = []
        for r in o.reads:
            if not r.multi:
                if o.dma_key is None:
                    r.readers = [p for p in r.readers if p.dma_key is not None or p.eng != o.eng]
                r.readers.append(o)
        for pid, p in deps.items():
            if p is o:
                continue
            if p.dma_key is None and o.dma_key is None and p.eng == o.eng:
                if o.eng == "pe":
                    continue
                if pid not in raw:
                    continue
            p.signal = True
            o.deps.append(p)
        self.ops.append(o)
        if dma_key is None:
            self.last_eng[eng] = o
        else:
            if not group:
                self.last_dma[dma_key] = o
        return o

    def barrier(self):
        targets = [p for p in self.last_eng.values() if p is not None] + list(self.last_dma.values())
        for e in self.ENGS:
            o = Op(e, None, (), (), None, False)
            o.signal = False
            for p in targets:
                if p.dma_key is None and p.eng == e:
                    continue
                p.signal = True
                o.deps.append(p)
            self.ops.append(o)
        self.last_dma = {}

    def emit(self):
        nc = self.nc
        eng_cnt = {e: 0 for e in self.ENGS}
        dma_cnt = {}
        for o in self.ops:
            if o.dma_key is not None:
                dma_cnt[o.dma_key] = dma_cnt.get(o.dma_key, 0) + 16
                o.sem = ("dma", o.dma_key)
                o.ticket = dma_cnt[o.dma_key]
            elif o.signal:
                eng_cnt[o.eng] += 1
                o.sem = ("eng", o.eng)
                o.ticket = eng_cnt[o.eng]
        for o in self.ops:
            if o.dma_key is not None and o.group:
                o.ticket = dma_cnt[o.dma_key]
        sem_names = [("eng", e) for e in self.ENGS] + [("dma", k) for k in dma_cnt]
        self.n_sems = len(sem_names)
        with contextlib.ExitStack() as es:
            sems = {}
            for i, k in enumerate(sem_names):
                sems[k] = es.enter_context(nc.semaphore("s%d" % i))
            block = es.enter_context(nc.Block())
            per_eng = {e: [o for o in self.ops if o.eng == e] for e in self.ENGS}

            def run(e, eng_obj):
                waited = {}
                for o in per_eng[e]:
                    for p in o.deps:
                        if waited.get(p.sem, 0) < p.ticket:
                            eng_obj.wait_ge(sems[p.sem], p.ticket)
                            waited[p.sem] = p.ticket
                    if o.fn is None:
                        continue
                    ins = o.fn(eng_obj)
                    if o.signal:
                        ins.then_inc(sems[o.sem], 16 if o.dma_key is not None else 1)
                if e == "sp":
                    for k, v in dma_cnt.items():
                        if waited.get(("dma", k), 0) < v:
                            eng_obj.wait_ge(sems[("dma", k)], v)

            @block.tensor
            def _(t):
                run("pe", t)

            @block.scalar
            def _(a):
                run("act", a)

            @block.vector
            def _(v):
                run("dve", v)

            @block.gpsimd
            def _(g):
                run("pool", g)

            @block.sync
            def _(s):
                run("sp", s)
        return eng_cnt, dma_cnt


class Tile:
    __slots__ = ("ap", "res")

    def __init__(self, ap, res):
        self.ap = ap
        self.res = res


class Arena:
    def __init__(self, arena_ap, nelems):
        self.a = arena_ap
        self.n = nelems
        self.off = 0
        self.mark = 0
        self.cnt = 0

    def alloc(self, shape, dt, name=None):
        nel = int(np.prod(shape[1:]))
        nb = nel * (4 if dt == F32 else 2)
        nb = (nb + 63) // 64 * 64
        o = self.off
        self.off += nb // 2
        assert self.off <= self.n, "arena overflow %d > %d" % (self.off, self.n)
        ap = self.a[0:shape[0], o:o + (nel * 2 if dt == F32 else nel)]
        if dt == F32:
            ap = ap.bitcast(F32)
        if len(shape) == 3:
            ap = ap.rearrange("p (a b) -> p a b", a=shape[1])
        elif len(shape) == 4:
            ap = ap.rearrange("p (a b c) -> p a b c", a=shape[1], b=shape[2])
        self.cnt += 1
        return Tile(ap, Res(name or ("t%d" % self.cnt)))

    def set_mark(self):
        self.mark = self.off

    def reset(self):
        self.off = self.mark


def t5_bucket(rel):
    nb = NBUCK // 2
    ret = (rel > 0).astype(np.int32) * nb
    n = np.abs(rel)
    max_exact = nb // 2
    large = max_exact + (np.log(np.maximum(n, 1) / max_exact) / np.log(1024 / max_exact)
                         * (nb - max_exact)).astype(np.int32)
    large = np.minimum(large, nb - 1)
    return (ret + np.where(n < max_exact, n, large)).astype(np.int32)


def onehot_tables():
    oh = np.zeros((3, 33, 384), np.float32)
    for g, r in enumerate(DIL_R):
        for i in range(384):
            jj = i - 191
            if -64 <= jj <= 64:
                b = int(t5_bucket(np.array([jj * r], dtype=np.int32))[0])
                oh[g, b, i] = 1.0
            else:
                oh[g, 32, i] = 1.0
    return oh


def rope_tables(pos):
    half = 32
    inv = (1.0 / (10000.0 ** (np.arange(half, dtype=np.float32) / half))).astype(np.float32)
    ang = pos.astype(np.float32)[None, :] * inv[:, None]
    cos = np.cos(ang).astype(np.float32)
    sin = np.sin(ang).astype(np.float32)
    cs = np.concatenate([cos, cos, cos, cos], axis=0)
    ss = np.concatenate([-sin, sin, -sin, sin], axis=0)
    return np.ascontiguousarray(cs), np.ascontiguousarray(ss)


def vb_cols(nq):
    return [nq // 128 + r for r in DIL_R]


def vbias_table(nq, q0, s_total):
    cols = []
    for g, r in enumerate(DIL_R):
        nb = nq // (128 * r)
        for c in range(r):
            for m in range(nb + 1):
                kk = np.arange(128)
                pidx = PAD + c - 64 * r + 128 * r * m + r * kk
                pos = q0 + pidx - PAD
                ok = (pos >= 0) & (pos < s_total)
                cols.append(np.where(ok, 0.0, MASKV).astype(np.float32))
    return np.ascontiguousarray(np.stack(cols, axis=1))


class JobSpec:
    def __init__(self, name, S, nq, nh):
        self.name = name
        self.S = S
        self.nq = nq
        self.nh = nh


def prep_job_inputs(spec, x_seq, mem, q0):
    s_total = x_seq.shape[0]
    nq = spec.nq
    assert spec.S == s_total
    order = np.concatenate([np.arange(q0, q0 + nq), np.arange(0, q0), np.arange(q0 + nq, s_total)])
    d = {}
    n = spec.name
    d["x_" + n] = np.ascontiguousarray(x_seq[order])
    if spec.nh:
        xh = np.zeros((2 * PAD, D), np.float32)
        lo = q0 - PAD
        for dst0, src0 in ((0, lo), (PAD, q0 + nq)):
            a = max(src0, 0)
            b = min(src0 + PAD, s_total)
            if b > a:
                xh[dst0 + (a - src0):dst0 + (b - src0)] = x_seq[a:b]
        d["xh_" + n] = xh
    d["mem_" + n] = np.ascontiguousarray(mem)
    cs, ss = rope_tables(order)
    d["cs_" + n] = cs
    d["ss_" + n] = ss
    d["vb_" + n] = vbias_table(nq, q0, s_total)
    return d


class Builder:
    def __init__(self, specs, dbg=False):
        self.specs = specs
        self.dbg = dbg
        nc = bass.Bass("TRN2", target_bir_lowering=False)
        self.nc = nc
        self.S = Sched(nc)
        self.ev = 0

    def dram_in(self, name, shape, dt=F32):
        return self.nc.dram_tensor(name, list(shape), dt, kind="ExternalInput").ap()

    def dram_out(self, name, shape, dt=F32):
        return self.nc.dram_tensor(name, list(shape), dt, kind="ExternalOutput").ap()

    def dram_scr(self, name, shape, dt=BF16):
        kind = "ExternalOutput" if (self.dbg and name in self.dbg) else "Internal"
        return self.nc.dram_tensor(name, list(shape), dt, kind=kind).ap()

    def declare(self):
        W = {}
        W["w_in"] = self.dram_in("w_in", [D, 12864])
        W["w_uq"] = self.dram_in("w_uq", [512, 2304])
        W["w_ukv"] = self.dram_in("w_ukv", [512, 3072])
        W["w_mem_kv"] = self.dram_in("w_mem_kv", [D, 2048])
        W["w_b_mla"] = self.dram_in("w_b_mla", [1536, D])
        W["w_b_dil"] = self.dram_in("w_b_dil", [512, D])
        W["w_b_mem"] = self.dram_in("w_b_mem", [1024, D])
        W["w_out"] = self.dram_in("w_out", [D, D])
        W["w_up"] = self.dram_in("w_up", [D, 8192])
        W["w_down"] = self.dram_in("w_down", [8192, D])
        for g in ("g_attn", "g_mem", "g_mlp", "g_final"):
            W[g] = self.dram_in(g, [1, D])
        W["g_q_norm"] = self.dram_in("g_q_norm", [1, 512])
        W["g_kv_norm"] = self.dram_in("g_kv_norm", [1, 512])
        W["rel_bias"] = self.dram_in("rel_bias", [NBUCK, H])
        W["oh"] = self.dram_in("oh", [3, 33, 384])
        W["antiid"] = self.dram_in("antiid", [128, 128])
        W["ident"] = self.dram_in("ident", [128, 128], BF16)
        self.W = W
        Z = {}
        Z["q"] = self.dram_scr("z_q", [D, 512])
        Z["kv"] = self.dram_scr("z_kv", [D, 512])
        Z["kr"] = self.dram_scr("z_kr", [D, 256])
        Z["dil"] = self.dram_scr("z_dil", [D, 4608])
        Z["xq"] = self.dram_scr("z_xq", [D, 1024])
        Z["g"] = self.dram_scr("z_g", [D, 16, 3, 128])
        Z["uqn"] = self.dram_scr("z_uqn", [512, 12, 128])
        Z["uqr"] = self.dram_scr("z_uqr", [512, 12, 64])
        Z["uqrs"] = self.dram_scr("z_uqrs", [512, 12, 64])
        Z["ukvk"] = self.dram_scr("z_ukvk", [512, 12, 128])
        Z["ukvv"] = self.dram_scr("z_ukvv", [512, 12, 128])
        Z["mkv"] = self.dram_scr("z_mkv", [D, 2048])
        Z["b"] = self.dram_scr("z_b", [3072, D])
        Z["out"] = self.dram_scr("z_out", [D, D])
        Z["up"] = self.dram_scr("z_up", [D, 8192])
        Z["down"] = self.dram_scr("z_down", [8192, D])
        self.Z = Z
        self.vp = self.dram_scr("vp", [12, 384], F32)
        self.Tsc = self.dram_scr("Tsc", [128, 2, 24, 128])
        self.jobs = []
        for sp in self.specs:
            n = sp.name
            J = {"spec": sp}
            J["x"] = self.dram_in("x_" + n, [sp.S, D])
            if sp.nh:
                J["xh"] = self.dram_in("xh_" + n, [sp.nh, D])
            J["mem"] = self.dram_in("mem_" + n, [256, D])
            J["cs"] = self.dram_in("cs_" + n, [128, sp.S])
            J["ss"] = self.dram_in("ss_" + n, [128, sp.S])
            J["vb"] = self.dram_in("vb_" + n, [128, sum(vb_cols(sp.nq))])
            J["y"] = self.dram_out("y_" + n, [sp.nq, D])
            J["QnT"] = self.dram_scr("QnT_" + n, [12, 128, sp.nq])
            J["QrT"] = self.dram_scr("QrT_" + n, [6, 128, sp.nq])
            J["KnT"] = self.dram_scr("KnT_" + n, [12, 128, sp.S])
            J["KrT"] = self.dram_scr("KrT_" + n, [128, sp.S])
            J["V"] = self.dram_scr("V_" + n, [12, 128, sp.S // 128, 128])
            J["dQT"] = self.dram_scr("dQT_" + n, [12, 128, sp.nq])
            J["dKT"] = self.dram_scr("dKT_" + n, [12, 128, sp.nq + 2 * PAD])
            J["dV"] = self.dram_scr("dV_" + n, [sp.nq + 2 * PAD, 1536])
            J["OmT"] = self.dram_scr("OmT_" + n, [12, 128, sp.nq])
            J["OdT"] = self.dram_scr("OdT_" + n, [4, 128, sp.nq])
            self.jobs.append(J)

    def load(self, dst, src_ap, key, reads=(), eng="sp", dst_ap=None, slow=False):
        d = dst.ap if dst_ap is None else dst_ap
        if slow:
            self.S.op(eng, lambda e: e.dma_start(out=d, in_=src_ap, allow_slow_non_contiguous=True), reads=reads,
                      writes=[dst.res], dma_key=key)
        else:
            self.S.op(eng, lambda e: e.dma_start(out=d, in_=src_ap), reads=reads, writes=[dst.res], dma_key=key)

    def store(self, dst_ap, src, key, writes=(), eng="pool", src_ap=None):
        s = src.ap if src_ap is None else src_ap
        self.S.op(eng, lambda e: e.dma_start(out=dst_ap, in_=s), reads=[src.res], writes=writes, dma_key=key)

    def mm(self, out_ap, out_res, lhsT_ap, rhs_ap, reads, start, stop):
        self.S.op("pe", lambda e: e.matmul(out_ap, lhsT=lhsT_ap, rhs=rhs_ap, start=start, stop=stop),
                  reads=reads, writes=[out_res])

    def evac(self, out_ap, out_res, in_ap, in_res, extra_reads=()):
        self.ev += 1
        if self.ev % 2 == 0:
            self.S.op("act", lambda e: e.copy(out=out_ap, in_=in_ap), reads=[in_res, *extra_reads], writes=[out_res])
        else:
            self.S.op("dve", lambda e: e.tensor_copy(out=out_ap, in_=in_ap), reads=[in_res, *extra_reads],
                      writes=[out_res])

    def bank(self):
        b = self.pring[self.pi % len(self.pring)]
        self.pi += 1
        return b

    def ring(self, name):
        lst = self.rings[name]
        i = self.ring_i.get(name, 0)
        self.ring_i[name] = i + 1
        return lst[i % len(lst)], "%s%d" % (name, i % len(lst))

    def mkring(self, name, n, shape, dt):
        self.rings[name] = [self.A.alloc(shape, dt, "%s%d" % (name, i)) for i in range(n)]
        self.ring_i[name] = 0

    def wblock(self, src_ap, kc, ncols, reads):
        t, key = self.ring("wr")
        v = t.ap[:, 0:kc * ncols].rearrange("p (c n) -> p c n", c=kc)
        self.S.op("sp", lambda e: e.dma_start(out=v, in_=src_ap), reads=reads, writes=[t.res], dma_key=key)
        return v, t.res

    def conv(self, dst, src, group):
        self.S.op("pool", lambda e: e.dma_start(out=dst, in_=src), writes=[self.r_w[group]],
                  dma_key="cv" + group, group=True)

    def conv_rows(self, dst, src, group, rows, step):
        for r0 in range(0, rows, step):
            self.conv(dst[r0:r0 + step], src[r0:r0 + step], group)

    def phase_W_early(self):
        W, Z = self.W, self.Z
        self.r_w = {"A1": Res("wA1", multi=True), "A2": Res("wA2", multi=True), "B": Res("wB", multi=True)}
        win = W["w_in"]
        self.conv(Z["q"], win[:, 0:512], "A1")
        self.conv(Z["kv"], win[:, 512:1024], "A1")
        self.conv(Z["kr"][:, 0:64], win[:, 1024:1088], "A1")
        self.conv(Z["kr"][:, 64:128], win[:, 1024:1088], "A1")
        for o in (128, 192):
            self.conv(Z["kr"][:, o:o + 32], win[:, 1056:1088], "A1")
            self.conv(Z["kr"][:, o + 32:o + 64], win[:, 1024:1056], "A1")
        uq = W["w_uq"].rearrange("k (h d) -> k h d", d=192)
        self.conv(Z["uqn"], uq[:, :, 0:128], "A1")
        self.conv(Z["uqr"], uq[:, :, 128:192], "A1")
        self.conv(Z["uqrs"][:, :, 0:32], uq[:, :, 160:192], "A1")
        self.conv(Z["uqrs"][:, :, 32:64], uq[:, :, 128:160], "A1")
        ukv = W["w_ukv"].rearrange("k (h d) -> k h d", d=256)
        self.conv(Z["ukvk"], ukv[:, :, 0:128], "A1")
        self.conv(Z["ukvv"], ukv[:, :, 128:256], "A1")
        self.conv_late = lambda: self.conv_rows(Z["dil"], win[:, 1088:5696], "A2", D, 512)
        pend = []

        def add(dst, src, rows, step):
            for r0 in range(0, rows, step):
                pend.append((dst[r0:r0 + step], src[r0:r0 + step]))
        add(Z["xq"], win[:, 5696:6720], D, 512)
        for r0 in range(0, D, 256):
            for br in range(3):
                gsrc = win[r0:r0 + 256, 6720 + br * 2048:6720 + (br + 1) * 2048].rearrange("k (fc d) -> k fc d", fc=16)
                pend.append((Z["g"][r0:r0 + 256, :, br, :], gsrc))
        add(Z["mkv"], W["w_mem_kv"], D, 256)
        add(Z["b"][0:1536], W["w_b_mla"], 1536, 256)
        add(Z["b"][1536:2048], W["w_b_dil"], 512, 256)
        add(Z["b"][2048:3072], W["w_b_mem"], 1024, 256)
        add(Z["out"], W["w_out"], D, 256)
        add(Z["up"], W["w_up"], D, 64)
        add(Z["down"], W["w_down"], 8192, 256)
        self.pend = pend

    def drip(self, n):
        for _ in range(n):
            if self.pend:
                d, s = self.pend.pop(0)
                self.conv(d, s, "B")

    def norm_T(self, src_rows, gb, hT, xtiles=None):
        S = self.S
        for j in range(4):
            if xtiles is None:
                xt, key = self.ring("xr")
            else:
                xt, key = xtiles[j], "xres%d" % j
            self.load(xt, src_rows[j * 128:(j + 1) * 128, :], key)
            self.norm_T_sub(xt, gb, hT, j)

    def norm_T_sub(self, xt, gb, hT, j, ncols=512):
        S = self.S
        st, _ = self.ring("stat")
        hb, _ = self.ring("hb")
        junk = self.junk
        S.op("act", lambda e: e.activation(out=junk.ap, in_=xt.ap, func=AF.Square, scale=float(D ** -0.5),
                                           accum_out=st.ap[:, 0:1]),
             reads=[xt.res], writes=[junk.res, st.res])
        S.op("act", lambda e: e.activation(out=st.ap[:, 1:2], in_=st.ap[:, 0:1], func=AF.Sqrt, bias=EPS),
             reads=[st.res], writes=[st.res])
        S.op("dve", lambda e: e.reciprocal(out=st.ap[:, 2:3], in_=st.ap[:, 1:2]), reads=[st.res], writes=[st.res])
        S.op("dve", lambda e: e.scalar_tensor_tensor(out=hb.ap, in0=xt.ap, scalar=st.ap[:, 2:3], in1=gb.ap,
                                                     op0=ALU.mult, op1=ALU.mult),
             reads=[xt.res, st.res, gb.res], writes=[hb.res])
        for half in range(2):
            tb = self.tbanks[self.ti % 2]
            self.ti += 1
            for c in range(8):
                cc = half * 8 + c
                S.op("pe", lambda e, tb=tb, c=c, cc=cc: e.transpose(out=tb.ap[:, c, :],
                                                                   in_=hb.ap[:, cc * 128:(cc + 1) * 128],
                                                                   identity=self.ident.ap),
                     reads=[hb.res, self.ident.res], writes=[tb.res])
            self.evac(hT.ap[:, half * 8:(half + 1) * 8, j * 128:(j + 1) * 128], hT.res, tb.ap, tb.res)

    def norm_pre(self, src_rows, gb):
        S = self.S
        hbs = []
        for j in range(4):
            xt, key = self.ring("xr")
            self.load(xt, src_rows[j * 128:(j + 1) * 128, :], key)
            st, _ = self.ring("stat")
            hb, _ = self.ring("hb")
            S.op("act", lambda e, hb=hb, xt=xt, st=st: e.activation(out=hb.ap, in_=xt.ap, func=AF.Square,
                                                                    scale=float(D ** -0.5), accum_out=st.ap[:, 0:1]),
                 reads=[xt.res], writes=[hb.res, st.res])
            S.op("act", lambda e, st=st: e.activation(out=st.ap[:, 1:2], in_=st.ap[:, 0:1], func=AF.Sqrt, bias=EPS),
                 reads=[st.res], writes=[st.res])
            S.op("dve", lambda e, st=st: e.reciprocal(out=st.ap[:, 2:3], in_=st.ap[:, 1:2]), reads=[st.res],
                 writes=[st.res])
            S.op("dve", lambda e, hb=hb, xt=xt, st=st: e.scalar_tensor_tensor(out=hb.ap, in0=xt.ap, scalar=st.ap[:, 2:3],
                                                                            in1=gb.ap, op0=ALU.mult, op1=ALU.mult),
                 reads=[xt.res, st.res, gb.res], writes=[hb.res])
            hbs.append(hb)
        return hbs

    def norm_post(self, hbs, hT):
        S = self.S
        for j, hb in enumerate(hbs):
            for half in range(2):
                tb = self.tbanks[self.ti % 2]
                self.ti += 1
                for c in range(8):
                    cc = half * 8 + c
                    S.op("pe", lambda e, tb=tb, c=c, cc=cc, hb=hb: e.transpose(out=tb.ap[:, c, :],
                                                                              in_=hb.ap[:, cc * 128:(cc + 1) * 128],
                                                                              identity=self.ident.ap),
                         reads=[hb.res, self.ident.res], writes=[tb.res])
                self.evac(hT.ap[:, half * 8:(half + 1) * 8, j * 128:(j + 1) * 128], hT.res, tb.ap, tb.res)

    def setup_norm2(self):
        self.mkring("xr", 2, [128, D], F32)
        self.mkring("stat", 4, [128, 4], F32)
        self.mkring("hb", 4, [128, D], BF16)

    def latent_proj(self, hT, wv, wres, c32, csq):
        S = self.S
        for oc in range(4):
            b = self.bank()
            for k in range(16):
                self.mm(b.ap, b.res, wv[:, k, oc * 128:(oc + 1) * 128], hT.ap[:, k, :], [wres, hT.res], k == 0, k == 15)
            S.op("dve", lambda e, b=b, oc=oc: e.tensor_copy(out=c32.ap[:, oc, :], in_=b.ap), reads=[b.res],
                 writes=[c32.res])
            S.op("act", lambda e, oc=oc: e.activation(out=csq.ap[:, oc, :], in_=c32.ap[:, oc, :], func=AF.Square),
                 reads=[c32.res], writes=[csq.res])

    def latent_norm(self, gvec, c32, csq, rsb, cn):
        S = self.S
        b = self.bank()
        for oc in range(4):
            self.mm(b.ap, b.res, self.ones.ap, csq.ap[:, oc, :], [self.ones.res, csq.res], oc == 0, oc == 3)
        S.op("act", lambda e: e.activation(out=rsb.ap, in_=b.ap, func=AF.Sqrt, bias=EPS, scale=1.0 / 512),
             reads=[b.res], writes=[rsb.res])
        S.op("dve", lambda e: e.reciprocal(out=rsb.ap, in_=rsb.ap), reads=[rsb.res], writes=[rsb.res])
        for oc in range(4):
            S.op("dve", lambda e, oc=oc: e.scalar_tensor_tensor(out=cn.ap[:, oc, :], in0=c32.ap[:, oc, :],
                                                                scalar=gvec.ap[:, oc:oc + 1], in1=rsb.ap,
                                                                op0=ALU.mult, op1=ALU.mult),
                 reads=[c32.res, gvec.res, rsb.res], writes=[cn.res])

    def begin_phase(self, pring_n=6):
        self.A.reset()
        self.rings = {}
        self.ring_i = {}
        self.pring = self.banks[0:pring_n]
        self.pi = 0
        self.ti = 0

    def setup_norm(self, nx=2):
        self.mkring("xr", nx, [128, D], F32)
        self.mkring("stat", 2, [128, 4], F32)
        self.mkring("hb", 2, [128, D], BF16)
        self.junk = self.A.alloc([128, D], BF16, "junk")

    def latent(self, hT, wv, wres, gvec, c32, csq, rsb, cn):
        S = self.S
        for oc in range(4):
            b = self.bank()
            for k in range(16):
                self.mm(b.ap, b.res, wv[:, k, oc * 128:(oc + 1) * 128], hT.ap[:, k, :], [wres, hT.res], k == 0, k == 15)
            S.op("dve", lambda e, b=b, oc=oc: e.tensor_copy(out=c32.ap[:, oc, :], in_=b.ap), reads=[b.res],
                 writes=[c32.res])
            S.op("act", lambda e, oc=oc: e.activation(out=csq.ap[:, oc, :], in_=c32.ap[:, oc, :], func=AF.Square),
                 reads=[c32.res], writes=[csq.res])
        b = self.bank()
        for oc in range(4):
            self.mm(b.ap, b.res, self.ones.ap, csq.ap[:, oc, :], [self.ones.res, csq.res], oc == 0, oc == 3)
        S.op("act", lambda e: e.activation(out=rsb.ap, in_=b.ap, func=AF.Sqrt, bias=EPS, scale=1.0 / 512),
             reads=[b.res], writes=[rsb.res])
        S.op("dve", lambda e: e.reciprocal(out=rsb.ap, in_=rsb.ap), reads=[rsb.res], writes=[rsb.res])
        for oc in range(4):
            S.op("dve", lambda e, oc=oc: e.scalar_tensor_tensor(out=cn.ap[:, oc, :], in0=c32.ap[:, oc, :],
                                                                scalar=gvec.ap[:, oc:oc + 1], in1=rsb.ap,
                                                                op0=ALU.mult, op1=ALU.mult),
                 reads=[c32.res, gvec.res, rsb.res], writes=[cn.res])

    def rope(self, ba, bb, cst, sst, dst_ap, key_dst):
        S = self.S
        t1, _ = self.ring("rt")
        t2, _ = self.ring("rt")
        so, skey = self.ring("st")
        S.op("dve", lambda e: e.tensor_tensor(out=t1.ap, in0=ba.ap, in1=cst.ap, op=ALU.mult),
             reads=[ba.res, cst.res], writes=[t1.res])
        S.op("dve", lambda e: e.tensor_tensor(out=t2.ap, in0=bb.ap, in1=sst.ap, op=ALU.mult),
             reads=[bb.res, sst.res], writes=[t2.res])
        S.op("pool", lambda e: e.tensor_tensor(out=so.ap, in0=t1.ap, in1=t2.ap, op=ALU.add),
             reads=[t1.res, t2.res], writes=[so.res])
        self.store(dst_ap, so, skey)

    def phase_A1(self, J, first):
        sp = J["spec"]
        S, A, Z = self.S, self.A, self.Z
        self.begin_phase()
        self.setup_norm2()
        rw = self.r_w["A1"]

        def wres(name, shape, src):
            t = A.alloc(shape, BF16, name)
            self.load(t, src, "w_" + name, reads=[rw])
            return t
        wq = wres("wq", [128, 16, 512], Z["q"].rearrange("(c p) n -> p c n", p=128))
        wkv = wres("wkv", [128, 16, 512], Z["kv"].rearrange("(c p) n -> p c n", p=128))
        wkr = wres("wkr", [128, 16, 256], Z["kr"].rearrange("(c p) n -> p c n", p=128))
        wukvk = wres("wukvk", [128, 4, 1536], Z["ukvk"].rearrange("(c p) h d -> p c (h d)", p=128))
        wukvv = wres("wukvv", [128, 4, 1536], Z["ukvv"].rearrange("(c p) h d -> p c (h d)", p=128))
        wuqn = wres("wuqn", [128, 4, 1536], Z["uqn"].rearrange("(c p) h d -> p c (h d)", p=128))
        wuqr = wres("wuqr", [128, 4, 768], Z["uqr"].rearrange("(c p) h d -> p c (h d)", p=128))
        wuqrs = wres("wuqrs", [128, 4, 768], Z["uqrs"].rearrange("(c p) h d -> p c (h d)", p=128))
        gq = A.alloc([128, 4], F32, "gq")
        gkv = A.alloc([128, 4], F32, "gkv")
        self.load(gq, self.W["g_q_norm"].rearrange("o (c p) -> p (o c)", p=128), "gq", slow=True)
        self.load(gkv, self.W["g_kv_norm"].rearrange("o (c p) -> p (o c)", p=128), "gkv", slow=True)
        gb = A.alloc([128, D], F32, "gb")
        self.load(gb, self.W["g_attn"].partition_broadcast(128), "gb")
        hts = [A.alloc([128, 16, 512], BF16, "hT%d" % i) for i in range(2)]
        cst = A.alloc([128, 512], F32, "cst")
        sst = A.alloc([128, 512], F32, "sst")
        c32 = A.alloc([128, 4, 512], F32, "c32")
        csq = A.alloc([128, 4, 512], BF16, "csq")
        rsb = A.alloc([128, 512], F32, "rsb")
        cnkv = A.alloc([128, 4, 512], BF16, "cnkv")
        cnq = A.alloc([128, 4, 512], BF16, "cnq")
        self.mkring("st", 4, [128, 512], BF16)
        self.mkring("stv", 2, [128, 1536], BF16)
        self.mkring("rt", 3, [128, 512], F32)
        ntile = sp.S // 512
        self.norm_post(self.norm_pre(J["x"][0:512, :], gb), hts[0])
        for i in range(ntile):
            t0 = 512 * i
            isq = t0 < sp.nq
            hT = hts[i % 2]
            nxt = self.norm_pre(J["x"][t0 + 512:t0 + 1024, :], gb) if i + 1 < ntile else None
            self.load(cst, J["cs"][:, t0:t0 + 512], "cst")
            self.load(sst, J["ss"][:, t0:t0 + 512], "sst")
            self.latent_proj(hT, wkv.ap, wkv.res, c32, csq)
            ba = self.bank()
            for k in range(16):
                self.mm(ba.ap, ba.res, wkr.ap[:, k, 0:128], hT.ap[:, k, :], [wkr.res, hT.res], k == 0, k == 15)
            bb = self.bank()
            for k in range(16):
                self.mm(bb.ap, bb.res, wkr.ap[:, k, 128:256], hT.ap[:, k, :], [wkr.res, hT.res], k == 0, k == 15)
            self.rope(ba, bb, cst, sst, J["KrT"][:, t0:t0 + 512], None)
            self.latent_norm(gkv, c32, csq, rsb, cnkv)
            if nxt is not None:
                self.norm_post(nxt, hts[(i + 1) % 2])
            for h in range(H):
                b = self.bank()
                for k in range(4):
                    self.mm(b.ap, b.res, wukvk.ap[:, k, h * 128:(h + 1) * 128], cnkv.ap[:, k, :],
                            [wukvk.res, cnkv.res], k == 0, k == 3)
                so, skey = self.ring("st")
                self.evac(so.ap, so.res, b.ap, b.res)
                self.store(J["KnT"][h, :, t0:t0 + 512], so, skey)

            def emit_v(t0=t0):
                for j in range(4):
                    sv, vkey = self.ring("stv")
                    for vb in range(3):
                        b = self.bank()
                        for k in range(4):
                            self.mm(b.ap, b.res, cnkv.ap[:, k, j * 128:(j + 1) * 128], wukvv.ap[:, k, vb * 512:(vb + 1) * 512],
                                    [wukvv.res, cnkv.res], k == 0, k == 3)
                        self.evac(sv.ap[:, vb * 512:(vb + 1) * 512], sv.res, b.ap, b.res)
                    ch = (t0 + 128 * j) // 128
                    self.store(J["V"][:, :, ch, :].rearrange("h p d -> p h d"), sv, vkey,
                               src_ap=sv.ap.rearrange("p (h d) -> p h d", h=12))
            if isq:
                self.latent_proj(hT, wq.ap, wq.res, c32, csq)
                self.latent_norm(gq, c32, csq, rsb, cnq)
                emit_v()
                for h in range(H):
                    b = self.bank()
                    for k in range(4):
                        self.mm(b.ap, b.res, wuqn.ap[:, k, h * 128:(h + 1) * 128], cnq.ap[:, k, :],
                                [wuqn.res, cnq.res], k == 0, k == 3)
                    so, skey = self.ring("st")
                    self.evac(so.ap, so.res, b.ap, b.res)
                    self.store(J["QnT"][h, :, t0:t0 + 512], so, skey)
                for pr in range(6):
                    ba = self.bank()
                    for k in range(4):
                        self.mm(ba.ap, ba.res, wuqr.ap[:, k, pr * 128:(pr + 1) * 128], cnq.ap[:, k, :],
                                [wuqr.res, cnq.res], k == 0, k == 3)
                    bb = self.bank()
                    for k in range(4):
                        self.mm(bb.ap, bb.res, wuqrs.ap[:, k, pr * 128:(pr + 1) * 128], cnq.ap[:, k, :],
                                [wuqrs.res, cnq.res], k == 0, k == 3)
                    self.rope(ba, bb, cst, sst, J["QrT"][pr, :, t0:t0 + 512], None)
            else:
                emit_v()
        S.barrier()

    def phase_A2(self, J, first):
        sp = J["spec"]
        S, A, Z = self.S, self.A, self.Z
        self.begin_phase()
        self.setup_norm2()
        rw = self.r_w["A2"]
        gb = A.alloc([128, D], F32, "gb")
        self.load(gb, self.W["g_attn"].partition_broadcast(128), "gb")
        hts = [A.alloc([128, 16, 512], BF16, "hT%d" % i) for i in range(2)]
        self.mkring("wr", 4, [128, 8192], BF16)
        self.mkring("st", 4, [128, 512], BF16)
        self.mkring("stv", 8, [128, 1536], BF16)
        zdil = Z["dil"].rearrange("(c p) n -> p c n", p=128)
        tiles = [("Q", J["x"][512 * i:512 * i + 512, :], PAD + 512 * i, 512 * i) for i in range(sp.nq // 512)]
        if sp.nh:
            for k in range(sp.nh // 512):
                po = 512 * k if k < 2 else sp.nq + PAD + 512 * (k - 2)
                tiles.append(("H", J["xh"][512 * k:512 * k + 512, :], po, None))
        else:
            z = A.alloc([128, PAD], BF16, "zero")
            S.op("dve", lambda e: e.memset(z.ap, 0.0), writes=[z.res])
            for gh in range(12):
                for po in (0, PAD + sp.nq):
                    self.store(J["dKT"][gh, :, po:po + PAD], z, "zst")
            for po in (0, PAD + sp.nq):
                for rb in range(PAD // 128):
                    self.store(J["dV"][po + rb * 128:po + rb * 128 + 128, 0:1024], z, "zst", src_ap=z.ap[:, 0:1024])
                    self.store(J["dV"][po + rb * 128:po + rb * 128 + 128, 1024:1536], z, "zst", src_ap=z.ap[:, 0:512])
        self.norm_post(self.norm_pre(tiles[0][1], gb), hts[0])
        for ti, (kind, src, po, qo) in enumerate(tiles):
            hT = hts[ti % 2]
            nxt = self.norm_pre(tiles[ti + 1][1], gb) if ti + 1 < len(tiles) else None
            svs = [self.ring("stv") for _ in range(4)]
            for blk in range(9):
                which, g = blk // 3, blk % 3
                if blk == 6 and nxt is not None:
                    self.norm_post(nxt, hts[(ti + 1) % 2])
                if kind == "H" and which == 0:
                    continue
                col0 = which * 1536 + g * 512
                wv, wr = self.wblock(zdil[:, :, col0:col0 + 512], 16, 512, [rw])

                if which < 2:
                    for c in range(4):
                        b = self.bank()
                        for k in range(16):
                            self.mm(b.ap, b.res, wv[:, k, c * 128:(c + 1) * 128], hT.ap[:, k, :], [wr, hT.res],
                                    k == 0, k == 15)
                        so, skey = self.ring("st")
                        self.evac(so.ap, so.res, b.ap, b.res)
                        gh = g * 4 + c
                        if which == 0:
                            self.store(J["dQT"][gh, :, qo:qo + 512], so, skey)
                        else:
                            self.store(J["dKT"][gh, :, po:po + 512], so, skey)
                else:
                    for j in range(4):
                        b = self.bank()
                        for k in range(16):
                            self.mm(b.ap, b.res, hT.ap[:, k, j * 128:(j + 1) * 128], wv[:, k, :], [wr, hT.res],
                                    k == 0, k == 15)
                        sv, vkey = svs[j]
                        self.evac(sv.ap[:, g * 512:(g + 1) * 512], sv.res, b.ap, b.res)
            for j in range(4):
                sv, vkey = svs[j]
                self.store(J["dV"][po + 128 * j:po + 128 * j + 128, :], sv, vkey)
        S.barrier()

    def phase_M(self, J, first):
        sp = J["spec"]
        S, A = self.S, self.A
        self.begin_phase(5)
        nkc = sp.S // 128
        krA = A.alloc([128, sp.S], BF16, "krA")
        krB = A.alloc([128, sp.S], BF16, "krB")
        S.op("dve", lambda e: e.memset(krA.ap[64:128, :], 0.0), writes=[krA.res])
        S.op("dve", lambda e: e.memset(krB.ap[0:64, :], 0.0), writes=[krB.res])
        self.load(krA, J["KrT"][0:64, :], "krA", dst_ap=krA.ap[0:64, :])
        self.load(krB, J["KrT"][64:128, :], "krB", dst_ap=krB.ap[64:128, :])
        self.mkring("kn", 2, [128, sp.S], BF16)
        self.mkring("vh", 2, [128, nkc, 128], BF16)
        self.mkring("qn", 2, [128, 512], BF16)
        self.mkring("qr", 2, [128, 512], BF16)
        self.mkring("pt", 8, [128, 512], BF16)
        self.mkring("ps", 3, [128, 512], BF16)
        self.mkring("rc", 2, [128, 512], F32)
        self.mkring("st", 2, [128, 512], BF16)
        accs = [(self.banks[5], self.banks[7]), (self.banks[6], self.banks[7])]
        accsb = [(A.alloc([128, 512], F32, "accD%d" % i), A.alloc([128, 512], F32, "accP%d" % i)) for i in range(2)]
        ones32 = A.alloc([128, 128], F32, "ones32")
        S.op("dve", lambda e: e.memset(ones32.ap, 1.0), writes=[ones32.res])
        nqt = sp.nq // 512
        units = [(h, qt) for h in range(H) for qt in range(nqt)]
        blocks = [(ui, kc) for ui in range(len(units)) for kc in range(nkc)]
        ctx = {}
        kvh = {}

        def open_unit(ui):
            h, qt = units[ui]
            if qt == 0:
                kn, kkey = self.ring("kn")
                self.load(kn, J["KnT"][h], kkey)
                vh, vkey = self.ring("vh")
                self.load(vh, J["V"][h], vkey)
                kvh[h] = (kn, vh)
            kn, vh = kvh[h]
            q0 = qt * 512
            qn, qkey = self.ring("qn")
            self.load(qn, J["QnT"][h, :, q0:q0 + 512], qkey)
            qr, rkey = self.ring("qr")
            self.load(qr, J["QrT"][h // 2, :, q0:q0 + 512], rkey)
            bo, bs = accs[ui % 2]
            ctx[ui] = dict(h=h, q0=q0, kn=kn, vh=vh, qn=qn, qr=qr, bo=bo, bs=bs, accD=accsb[ui % 2][0],
                           kr=krA if h % 2 == 0 else krB, prev=None)

        def close_unit(ui):
            c = ctx.pop(ui)
            bo, bs, accD = c["bo"], c["bs"], c["accD"]
            self.mm(bs.ap, bs.res, ones32.ap, accD.ap, [ones32.res, accD.res], True, True)
            rc, _ = self.ring("rc")
            S.op("dve", lambda e: e.reciprocal(out=rc.ap, in_=bs.ap), reads=[bs.res], writes=[rc.res])
            so, skey = self.ring("st")
            S.op("dve", lambda e: e.tensor_tensor(out=so.ap, in0=bo.ap, in1=rc.ap, op=ALU.mult),
                 reads=[bo.res, rc.res], writes=[so.res])
            self.store(J["OmT"][c["h"], :, c["q0"]:c["q0"] + 512], so, skey)
            if first:
                self.drip(2)

        LA = 3
        pend = {}
        for step in range(len(blocks) + LA):
            if step < len(blocks):
                ui, kc = blocks[step]
                if kc == 0:
                    open_unit(ui)
                c = ctx[ui]
                b = self.bank()
                self.mm(b.ap, b.res, c["kn"].ap[:, kc * 128:(kc + 1) * 128], c["qn"].ap, [c["kn"].res, c["qn"].res],
                        True, False)
                self.mm(b.ap, b.res, c["kr"].ap[:, kc * 128:(kc + 1) * 128], c["qr"].ap, [c["kr"].res, c["qr"].res],
                        False, True)
                pt, _ = self.ring("pt")
                S.op("act", lambda e, b=b, pt=pt: e.activation(out=pt.ap, in_=b.ap, func=AF.Exp, scale=SC_MLA),
                     reads=[b.res], writes=[pt.res])
                pend[step] = pt
            k = step - LA
            if k >= 0:
                ui, kc = blocks[k]
                c = ctx[ui]
                pt = pend.pop(k)
                bo, vh, accD = c["bo"], c["vh"], c["accD"]
                self.mm(bo.ap, bo.res, vh.ap[:, kc, :], pt.ap, [vh.res, pt.res], kc == 0, kc == nkc - 1)
                if kc % 2 == 0:
                    c["prev"] = pt
                else:
                    ptprev = c["prev"]
                    ps_, _ = self.ring("ps")
                    S.op("dve", lambda e, ps_=ps_, a=ptprev, b2=pt: e.tensor_tensor(out=ps_.ap, in0=a.ap, in1=b2.ap,
                                                                                  op=ALU.add),
                         reads=[ptprev.res, pt.res], writes=[ps_.res])
                    if kc == 1:
                        S.op("dve", lambda e, ps_=ps_, ac=accD: e.tensor_copy(out=ac.ap, in_=ps_.ap),
                             reads=[ps_.res], writes=[accD.res])
                    else:
                        S.op("dve", lambda e, ps_=ps_, ac=accD: e.tensor_tensor(out=ac.ap, in0=ps_.ap, in1=ac.ap,
                                                                              op=ALU.add),
                             reads=[ps_.res, accD.res], writes=[accD.res])
                if kc == nkc - 1:
                    close_unit(ui)
        S.barrier()

    def phase_T(self):
        S, A, W = self.S, self.A, self.W
        self.begin_phase()
        rb = A.alloc([33, 12], F32, "rb")
        S.op("dve", lambda e: e.memset(rb.ap, MASKV), writes=[rb.res])
        self.load(rb, W["rel_bias"], "rb", dst_ap=rb.ap[0:32, :])
        S.op("dve", lambda e: e.tensor_scalar(out=rb.ap, in0=rb.ap, scalar1=float(1.0 / SC_DIL), scalar2=None,
                                              op0=ALU.mult), reads=[rb.res], writes=[rb.res])
        oh = A.alloc([33, 3, 384], F32, "oh")
        self.load(oh, W["oh"].rearrange("g b i -> b g i"), "oh")
        aid = A.alloc([128, 128], F32, "aid")
        self.load(aid, W["antiid"], "aid")
        vps = A.alloc([4, 3, 384], F32, "vps")
        for g in range(3):
            b = self.bank()
            self.mm(b.ap[0:4, 0:384], b.res, rb.ap[:, g * 4:(g + 1) * 4], oh.ap[:, g, :], [rb.res, oh.res], True, True)
            S.op("dve", lambda e, b=b, g=g: e.tensor_copy(out=vps.ap[:, g, :], in_=b.ap[0:4, 0:384]), reads=[b.res],
                 writes=[vps.res])
        r_vp = Res("vp", multi=True)
        self.store(self.vp.rearrange("(g h) i -> h g i", g=3), vps, "vps", writes=[r_vp])
        self.mkring("hk", 8, [128, 128], F32)
        self.mkring("t32", 4, [128, 128], F32)
        thl = A.alloc([128, 2, 24, 128], BF16, "thl")
        self.mkring("tt", 4, [128, 128], F32)
        for gh in range(12):
            for j in range(2):
                hk, hkey = self.ring("hk")
                src = bass.AP(self.vp.tensor, gh * 384 + 128 * j, [[1, 128], [1, 128]])
                self.load(hk, src, hkey, reads=[r_vp])
                b = self.bank()
                self.mm(b.ap[:, 0:128], b.res, hk.ap, aid.ap, [hk.res, aid.res], True, True)
                t32, _ = self.ring("t32")
                S.op("dve", lambda e, b=b, t32=t32: e.tensor_copy(out=t32.ap, in_=b.ap[:, 0:128]), reads=[b.res],
                     writes=[t32.res])
                idx = gh * 2 + j
                S.op("dve", lambda e, t32=t32, idx=idx: e.tensor_copy(out=thl.ap[:, 0, idx, :], in_=t32.ap),
                     reads=[t32.res], writes=[thl.res])
                tt, _ = self.ring("tt")
                S.op("dve", lambda e, t32=t32, idx=idx, tt=tt: e.tensor_tensor(out=tt.ap, in0=t32.ap,
                                                                            in1=thl.ap[:, 0, idx, :], op=ALU.subtract),
                     reads=[t32.res, thl.res], writes=[tt.res])
                S.op("dve", lambda e, tt=tt, idx=idx: e.tensor_copy(out=thl.ap[:, 1, idx, :], in_=tt.ap),
                     reads=[tt.res], writes=[thl.res])
        self.store(self.Tsc, thl, "thl")
        S.barrier()

    def phase_D(self, J):
        sp = J["spec"]
        S, A = self.S, self.A
        self.begin_phase(6)
        nq = sp.nq
        thl = A.alloc([128, 2, 24, 128], BF16, "thl")
        self.load(thl, self.Tsc, "thl")
        ncol = vb_cols(nq)
        vbt = A.alloc([128, sum(ncol)], F32, "vbt")
        self.load(vbt, J["vb"], "vbt")
        acc = A.alloc([128, 2, nq], F32, "acc")
        self.mkring("dq", 2, [128, nq], BF16)
        self.mkring("dk", 2, [128, nq + 2 * PAD], BF16)
        self.mkring("vt", 10, [128, 128], BF16)
        self.mkring("pt", 5, [128, 2, 128], BF16)
        self.mkring("st", 2, [128, nq], BF16)
        rcp = A.alloc([128, nq], F32, "rcp")
        for hh in range(4):
            for g, r in enumerate(DIL_R):
                gh = g * 4 + hh
                dq, qkey = self.ring("dq")
                self.load(dq, J["dQT"][gh], qkey)
                dk, kkey = self.ring("dk")
                self.load(dk, J["dKT"][gh], kkey)
                nb = nq // (128 * r)
                colbase = sum(ncol[:g])
                blocks = [(c, bq) for c in range(r) for bq in range(nb)]
                vts = {}

                def get_vt(c, m, r=r, gh=gh, vts=vts):
                    if (c, m) not in vts:
                        vt, vkey = self.ring("vt")
                        row0 = PAD + c - 64 * r + 128 * r * m
                        src = bass.AP(J["dV"].tensor, row0 * 1536 + gh * 128, [[r * 1536, 128], [1, 128]])
                        self.load(vt, src, vkey)
                        vts[(c, m)] = vt
                    return vts[(c, m)]
                LA = 2
                pend = {}
                for step in range(len(blocks) + LA):
                    if step < len(blocks):
                        c, bq = blocks[step]
                        vt0 = get_vt(c, bq)
                        vt1 = get_vt(c, bq + 1)
                        qs = c + r * 128 * bq
                        q_ap = dq.ap[:, qs:qs + 127 * r + 1:r]
                        b = self.bank()
                        b3 = b.ap.rearrange("p (a t) -> p a t", a=4)
                        for j in range(2):
                            ks = PAD + c - 64 * r + 128 * r * (bq + j)
                            k_ap = dk.ap[:, ks:ks + 127 * r + 1:r]
                            self.mm(b3[:, j, :], b.res, k_ap, q_ap, [dk.res, dq.res], True, False)
                            self.mm(b3[:, j, :], b.res, self.ident.ap, thl.ap[:, 0, gh * 2 + j, :],
                                    [self.ident.res, thl.res], False, False)
                            self.mm(b3[:, j, :], b.res, self.ident.ap, thl.ap[:, 1, gh * 2 + j, :],
                                    [self.ident.res, thl.res], False, True)
                        pt, _ = self.ring("pt")
                        for j in range(2):
                            col = colbase + c * (nb + 1) + bq + j
                            S.op("act", lambda e, b3=b3, pt=pt, j=j, col=col: e.activation(
                                out=pt.ap[:, j, :], in_=b3[:, j, :], func=AF.Exp, scale=SC_DIL,
                                bias=vbt.ap[:, col:col + 1]), reads=[b.res, vbt.res], writes=[pt.res])
                        pend[step] = (qs, pt, vt0, vt1)
                    k = step - LA
                    if k >= 0:
                        qs, pt, vt0, vt1 = pend.pop(k)
                        bo = self.bank()
                        o3 = bo.ap.rearrange("p (a t) -> p a t", a=4)
                        for j, vt in enumerate((vt0, vt1)):
                            self.mm(o3[:, 0, :], bo.res, vt.ap, pt.ap[:, j, :], [vt.res, pt.res], j == 0, j == 1)
                        for j in range(2):
                            self.mm(o3[:, 1, :], bo.res, self.ones.ap, pt.ap[:, j, :], [self.ones.res, pt.res],
                                    j == 0, j == 1)
                        a_ap = acc.ap[:, :, qs:qs + 127 * r + 1:r]
                        if g == 0:
                            S.op("dve", lambda e, a_ap=a_ap, o3=o3: e.tensor_copy(out=a_ap, in_=o3[:, 0:2, :]),
                                 reads=[bo.res], writes=[acc.res])
                        else:
                            S.op("dve", lambda e, a_ap=a_ap, o3=o3: e.tensor_tensor(out=a_ap, in0=o3[:, 0:2, :],
                                                                                 in1=a_ap, op=ALU.add),
                                 reads=[bo.res, acc.res], writes=[acc.res])
            S.op("dve", lambda e: e.reciprocal(out=rcp.ap, in_=acc.ap[:, 1, :]), reads=[acc.res], writes=[rcp.res])
            so, skey = self.ring("st")
            S.op("dve", lambda e, so=so: e.tensor_tensor(out=so.ap, in0=acc.ap[:, 0, :], in1=rcp.ap, op=ALU.mult),
                 reads=[acc.res, rcp.res], writes=[so.res])
            self.store(J["OdT"][hh], so, skey)
        S.barrier()

    def phase_B(self, J):
        sp = J["spec"]
        S, A, Z, W = self.S, self.A, self.Z, self.W
        self.begin_phase(6)
        rw = self.r_w["B"]
        self.mkring("stat", 4, [128, 4], F32)
        self.mkring("hb", 4, [128, D], BF16)
        self.junk = A.alloc([128, D], BF16, "junk")
        gb = A.alloc([128, D], F32, "gb")
        hT = A.alloc([128, 16, 512], BF16, "hT")
        self.mkring("wr", 3, [128, 8192], BF16)
        xres = [A.alloc([128, D], F32, "xres%d" % j) for j in range(4)]
        kmT = A.alloc([128, 8, 256], BF16, "kmT")
        vm = A.alloc([128, 2, 1024], BF16, "vm")
        self.load(gb, W["g_mem"].partition_broadcast(128), "gb")
        for j in range(2):
            self.load(xres[j], J["mem"][j * 128:(j + 1) * 128, :], "xres%d" % j)
            self.norm_T_sub(xres[j], gb, hT, j)
        zm = Z["mkv"].rearrange("(c p) n -> p c n", p=128)
        for blk in range(4):
            wv, wr = self.wblock(zm[:, :, blk * 512:(blk + 1) * 512], 16, 512, [rw])
            if blk < 2:
                for c in range(4):
                    b = self.bank()
                    for k in range(16):
                        self.mm(b.ap[:, 0:256], b.res, wv[:, k, c * 128:(c + 1) * 128], hT.ap[:, k, 0:256], [wr, hT.res],
                                k == 0, k == 15)
                    self.evac(kmT.ap[:, blk * 4 + c, :], kmT.res, b.ap[:, 0:256], b.res)
            else:
                for mc in range(2):
                    b = self.bank()
                    for k in range(16):
                        self.mm(b.ap, b.res, hT.ap[:, k, mc * 128:(mc + 1) * 128], wv[:, k, :], [wr, hT.res],
                                k == 0, k == 15)
                    self.evac(vm.ap[:, mc, (blk - 2) * 512:(blk - 1) * 512], vm.res, b.ap, b.res)
        big = A.alloc([128, 32, 512], BF16, "big")
        obr = Tile(big.ap[:, 0:24, :], Res("obr"))
        xq = Tile(big.ap[:, 24:32, :], Res("xq"))
        u = big
        ures = [obr.res, xq.res]
        mrg = A.alloc([128, 16, 512], BF16, "mrg")
        self.mkring("pt", 3, [128, 2, 512], BF16)
        self.mkring("rc", 2, [128, 512], F32)
        self.mkring("gt", 3, [128, 512], F32)
        self.mkring("mt", 3, [128, 512], F32)
        zxq = Z["xq"].rearrange("(c p) n -> p c n", p=128)
        zg = Z["g"].rearrange("(c p) fc br d -> p c fc (br d)", p=128)
        zb = Z["b"].rearrange("(c p) n -> p c n", p=128)
        zo = Z["out"].rearrange("(c p) n -> p c n", p=128)
        zu = Z["up"].rearrange("(c p) n -> p c n", p=128)
        zd = Z["down"].rearrange("(c p) n -> p c n", p=128)
        def npre(xt):
            st, _ = self.ring("stat")
            hb, _ = self.ring("hb")
            S.op("act", lambda e: e.activation(out=hb.ap, in_=xt.ap, func=AF.Square, scale=float(D ** -0.5),
                                               accum_out=st.ap[:, 0:1]), reads=[xt.res], writes=[hb.res, st.res])
            S.op("act", lambda e: e.activation(out=st.ap[:, 1:2], in_=st.ap[:, 0:1], func=AF.Sqrt, bias=EPS),
                 reads=[st.res], writes=[st.res])
            S.op("dve", lambda e: e.reciprocal(out=st.ap[:, 2:3], in_=st.ap[:, 1:2]), reads=[st.res], writes=[st.res])
            S.op("dve", lambda e: e.scalar_tensor_tensor(out=hb.ap, in0=xt.ap, scalar=st.ap[:, 2:3], in1=gb.ap,
                                                         op0=ALU.mult, op1=ALU.mult),
                 reads=[xt.res, st.res, gb.res], writes=[hb.res])
            return hb

        for i in range(sp.nq // 512):
            t0 = 512 * i
            self.load(gb, W["g_attn"].partition_broadcast(128), "gb")
            hbs = []
            for j in range(4):
                self.load(xres[j], J["x"][t0 + 128 * j:t0 + 128 * j + 128, :], "xres%d" % j)
                hbs.append(npre(xres[j]))
            self.norm_post(hbs, hT)
            for blk in range(2):
                wv, wr = self.wblock(zxq[:, :, blk * 512:(blk + 1) * 512], 16, 512, [rw])
                for c in range(4):
                    b = self.bank()
                    for k in range(16):
                        self.mm(b.ap, b.res, wv[:, k, c * 128:(c + 1) * 128], hT.ap[:, k, :], [wr, hT.res], k == 0, k == 15)
                    self.evac(xq.ap[:, blk * 4 + c, :], xq.res, b.ap, b.res)
            self.load(obr, J["OmT"][:, :, t0:t0 + 512].rearrange("h p t -> p h t"), "obr_m", dst_ap=obr.ap[:, 0:12, :])
            self.load(obr, J["OdT"][:, :, t0:t0 + 512].rearrange("h p t -> p h t"), "obr_d", dst_ap=obr.ap[:, 12:16, :])
            def mem_scores(hm):
                pt, _ = self.ring("pt")
                for mc in range(2):
                    bq_ = self.bank()
                    for dc in range(2):
                        self.mm(bq_.ap, bq_.res, kmT.ap[:, hm * 2 + dc, mc * 128:(mc + 1) * 128], xq.ap[:, hm * 2 + dc, :],
                                [kmT.res, xq.res], dc == 0, dc == 1)
                    S.op("act", lambda e, bq_=bq_, pt=pt, mc=mc: e.activation(out=pt.ap[:, mc, :], in_=bq_.ap, func=AF.Exp,
                                                                            scale=SC_MEM), reads=[bq_.res], writes=[pt.res])
                return pt

            def mem_pv(hm, pt):
                bs = self.bank()
                for mc in range(2):
                    self.mm(bs.ap, bs.res, self.ones.ap, pt.ap[:, mc, :], [self.ones.res, pt.res], mc == 0, mc == 1)
                rc, _ = self.ring("rc")
                S.op("dve", lambda e: e.reciprocal(out=rc.ap, in_=bs.ap), reads=[bs.res], writes=[rc.res])
                for dvc in range(2):
                    bv = self.bank()
                    for mc in range(2):
                        c0 = hm * 256 + dvc * 128
                        self.mm(bv.ap, bv.res, vm.ap[:, mc, c0:c0 + 128], pt.ap[:, mc, :], [vm.res, pt.res], mc == 0, mc == 1)
                    S.op("dve", lambda e, bv=bv, dvc=dvc: e.tensor_tensor(
                        out=obr.ap[:, 16 + hm * 2 + dvc, :], in0=bv.ap, in1=rc.ap, op=ALU.mult),
                        reads=[bv.res, rc.res], writes=[obr.res])
            pts_ = {0: mem_scores(0)}
            for hm in range(4):
                if hm + 1 < 4:
                    pts_[hm + 1] = mem_scores(hm + 1)
                mem_pv(hm, pts_.pop(hm))
            for fc in range(16):
                wg, wgr = self.wblock(zg[:, :, fc, :], 16, 384, [rw])
                if fc % 2 == 0:
                    wb, wbr = self.wblock(zb[:, :, fc * 128:(fc + 2) * 128], 24, 256, [rw])
                gts = []
                for br in range(3):
                    b = self.bank()
                    for k in range(16):
                        self.mm(b.ap, b.res, wg[:, k, br * 128:(br + 1) * 128], hT.ap[:, k, :], [wgr, hT.res], k == 0, k == 15)
                    gt, _ = self.ring("gt")
                    S.op("act", lambda e, b=b, gt=gt: e.activation(out=gt.ap, in_=b.ap, func=AF.Sigmoid),
                         reads=[b.res], writes=[gt.res])
                    gts.append(gt)
                mts = []
                for br, (k0, k1) in enumerate(((0, 12), (12, 16), (16, 24))):
                    b = self.bank()
                    for k in range(k0, k1):
                        self.mm(b.ap, b.res, wb[:, k, (fc % 2) * 128:(fc % 2 + 1) * 128], obr.ap[:, k, :], [wbr, obr.res],
                                k == k0, k == k1 - 1)
                    mt, _ = self.ring("mt")
                    S.op("dve", lambda e, b=b, mt=mt, gt=gts[br]: e.tensor_tensor(out=mt.ap, in0=b.ap, in1=gt.ap,
                                                                                op=ALU.mult),
                         reads=[b.res, gts[br].res], writes=[mt.res])
                    mts.append(mt)
                S.op("pool", lambda e, a=mts[0], b2=mts[1]: e.tensor_tensor(out=a.ap, in0=a.ap, in1=b2.ap, op=ALU.add),
                     reads=[mts[0].res, mts[1].res], writes=[mts[0].res])
                S.op("pool", lambda e, a=mts[0], b2=mts[2], fc=fc: e.tensor_tensor(out=mrg.ap[:, fc, :], in0=a.ap,
                                                                                in1=b2.ap, op=ALU.add),
                     reads=[mts[0].res, mts[2].res], writes=[mrg.res])
            hbs2 = []
            for fb in range(4):
                wv, wr = self.wblock(zo[:, :, fb * 512:(fb + 1) * 512], 16, 512, [rw])
                for j in range(4):
                    b = self.bank()
                    for k in range(16):
                        self.mm(b.ap, b.res, mrg.ap[:, k, j * 128:(j + 1) * 128], wv[:, k, :], [wr, mrg.res], k == 0, k == 15)
                    xs = xres[j].ap[:, fb * 512:(fb + 1) * 512]
                    S.op("dve", lambda e, b=b, xs=xs: e.tensor_tensor(out=xs, in0=b.ap, in1=xs, op=ALU.add),
                         reads=[b.res, xres[j].res], writes=[xres[j].res])
                    if fb == 3:
                        if j == 0:
                            self.load(gb, W["g_mlp"].partition_broadcast(128), "gb")
                        hbs2.append(npre(xres[j]))
            self.norm_post(hbs2, hT)
            for half in range(2):
                for fb in range(8):
                    col0 = half * 4096 + fb * 512
                    wv, wr = self.wblock(zu[:, :, col0:col0 + 512], 16, 512, [rw])
                    for c in range(4):
                        b = self.bank()
                        for k in range(16):
                            self.mm(b.ap, b.res, wv[:, k, c * 128:(c + 1) * 128], hT.ap[:, k, :], [wr, hT.res],
                                    k == 0, k == 15)
                        rl, _ = self.ring("gt")
                        S.op("act", lambda e, b=b, rl=rl: e.activation(out=rl.ap, in_=b.ap, func=AF.Relu),
                             reads=[b.res], writes=[rl.res])
                        S.op("dve", lambda e, rl=rl, fb=fb, c=c: e.tensor_tensor(out=u.ap[:, fb * 4 + c, :], in0=rl.ap,
                                                                                in1=rl.ap, op=ALU.mult),
                             reads=[rl.res], writes=ures)
                for fb in range(4):
                    wbs = []
                    for kb in range(2):
                        r0 = half * 32 + kb * 16
                        wbs.append(self.wblock(zd[:, r0:r0 + 16, fb * 512:(fb + 1) * 512], 16, 512, [rw]))
                    last = half == 1 and fb == 3
                    if last:
                        self.load(gb, W["g_final"].partition_broadcast(128), "gb")
                    for j in range(4):
                        bk = self.bank()
                        for kk in range(32):
                            wv, wr = wbs[kk // 16]
                            self.mm(bk.ap, bk.res, u.ap[:, kk, j * 128:(j + 1) * 128], wv[:, kk % 16, :], [wr, *ures],
                                    kk == 0, kk == 31)
                        xt = xres[j]
                        xs = xt.ap[:, fb * 512:(fb + 1) * 512]
                        S.op("dve", lambda e, bk=bk, xs=xs: e.tensor_tensor(out=xs, in0=bk.ap, in1=xs, op=ALU.add),
                             reads=[bk.res, xt.res], writes=[xt.res])
                        if last:
                            st, _ = self.ring("stat")
                            junk = self.junk
                            S.op("act", lambda e, st=st, xt=xt, junk=junk: e.activation(
                                out=junk.ap, in_=xt.ap, func=AF.Square, scale=float(D ** -0.5), accum_out=st.ap[:, 0:1]),
                                reads=[xt.res], writes=[junk.res, st.res])
                            S.op("act", lambda e, st=st: e.activation(out=st.ap[:, 1:2], in_=st.ap[:, 0:1], func=AF.Sqrt,
                                                                     bias=EPS), reads=[st.res], writes=[st.res])
                            S.op("dve", lambda e, st=st: e.reciprocal(out=st.ap[:, 2:3], in_=st.ap[:, 1:2]),
                                 reads=[st.res], writes=[st.res])
                            S.op("dve", lambda e, st=st, xt=xt: e.scalar_tensor_tensor(
                                out=xt.ap, in0=xt.ap, scalar=st.ap[:, 2:3], in1=gb.ap, op0=ALU.mult, op1=ALU.mult),
                                reads=[xt.res, st.res, gb.res], writes=[xt.res])
                            self.store(J["y"][t0 + 128 * j:t0 + 128 * j + 128, :], xt, "ys%d" % j)
        S.barrier()

    def build(self, phases=("T", "A1", "A2", "M", "D", "B")):
        nc = self.nc
        self.declare()
        with contextlib.ExitStack() as es:
            arena = es.enter_context(nc.sbuf_tensor("arena", [128, ARENA_ELEMS], BF16))
            ps = es.enter_context(nc.psum_tensor("ps", [128, 8, 512], F32))
            self.A = Arena(arena, ARENA_ELEMS)
            self.banks = [Tile(ps[:, i, :], Res("bank%d" % i)) for i in range(8)]
            self.tbanks = [Tile(ps[:, 6 + i, :].bitcast(BF16).rearrange("p (c t) -> p c t", c=8), self.banks[6 + i].res)
                           for i in range(2)]
            self.rings = {}
            self.ring_i = {}
            self.ident = self.A.alloc([128, 128], BF16, "ident")
            self.ones = self.A.alloc([128, 128], BF16, "ones")
            self.load(self.ident, self.W["ident"], "ident")
            self.S.op("dve", lambda e: e.memset(self.ones.ap, 1.0), writes=[self.ones.res])
            self.A.set_mark()
            self.phase_W_early()
            if "T" in phases:
                self.phase_T()
            self.conv_late()
            for ji, J in enumerate(self.jobs):
                first = ji == 0
                if "A1" in phases:
                    self.phase_A1(J, first)
                if "A2" in phases:
                    self.phase_A2(J, first)
                if "M" in phases:
                    self.phase_M(J, first)
                if first:
                    self.drip(1000)
                if "D" in phases:
                    self.phase_D(J)
                if "B" in phases:
                    self.phase_B(J)
            cnt = self.S.emit()
            self.cnt = cnt
        return nc


def common_inputs(inp):
    d = {}
    for k in ("w_in", "w_uq", "w_ukv", "w_mem_kv", "w_b_mla", "w_b_dil", "w_b_mem", "w_out", "w_up", "w_down"):
        d[k] = np.ascontiguousarray(np.asarray(inp[k], dtype=np.float32)[0])
    for k in ("g_attn", "g_mem", "g_mlp", "g_q_norm", "g_kv_norm"):
        d[k] = np.ascontiguousarray(np.asarray(inp[k], dtype=np.float32).reshape(1, -1))
    d["g_final"] = np.ascontiguousarray(np.asarray(inp["g_final"], dtype=np.float32).reshape(1, -1))
    d["rel_bias"] = np.ascontiguousarray(np.asarray(inp["rel_bias"], dtype=np.float32))
    d["oh"] = onehot_tables()
    d["antiid"] = np.ascontiguousarray(np.eye(128, dtype=np.float32)[::-1])
    d["ident"] = np.eye(128).astype(ml_dtypes.bfloat16)
    return d


_CACHE = {}


def kernel(**inputs):
    n = 8
    specs = [JobSpec("S", 4096, 4096, 0), JobSpec("P", 8192, 2048, 2 * PAD)]
    if "nc" not in _CACHE:
        _CACHE["nc"] = Builder(specs).build()
    nc = _CACHE["nc"]
    com = common_inputs(inputs)
    xs = np.asarray(inputs["x_sample"], dtype=np.float32)
    xp = np.asarray(inputs["x_prompt"], dtype=np.float32)
    ms = np.asarray(inputs["mem_sample"], dtype=np.float32)
    mp = np.asarray(inputs["mem_prompt"], dtype=np.float32)
    in_maps = []
    for c in range(n):
        d = dict(com)
        d.update(prep_job_inputs(specs[0], xs[c], ms[c], 0))
        d.update(prep_job_inputs(specs[1], xp[c // 4], mp[c // 4], (c % 4) * 2048))
        in_maps.append(d)
    res = run_bass_kernel_spmd(nc, in_maps, core_ids=list(range(n)))
    y_s = np.stack([np.asarray(res.results[c]["y_S"], dtype=np.float32) for c in range(n)], axis=0)
    y_p = np.zeros((2, 8192, D), np.float32)
    for c in range(n):
        y_p[c // 4, (c % 4) * 2048:(c % 4 + 1) * 2048] = np.asarray(res.results[c]["y_P"], dtype=np.float32)
    return (y_p, y_s)
```
